# Optimizing a Trainium2 kernel written in Bass

```python
import jax, jax.numpy as jnp
from jax import lax
import numpy as np

D_MODEL = 1024
BATCH = 4
SEQ = 4096
DEPTH = 1

HEAD_DIM = 64
W_CONV = D_MODEL // 2
W_ATTN = D_MODEL - W_CONV
N_CONV_GROUPS = W_CONV // HEAD_DIM
N_ATTN_HEADS = W_ATTN // HEAD_DIM
CONV_K = 3
D_FF = 4 * D_MODEL
PLE_DIM = 256
Q_BLOCK = 128
EPS = 1e-6
IN_COLS = 3 * W_CONV + 3 * W_ATTN

kernel_name = "hymba_shortconv_stickbreaking_hybrid"


def rmsnorm(x, g):
    xf = x.astype(jnp.float32)
    y = xf * lax.rsqrt(jnp.mean(xf * xf, axis=-1, keepdims=True) + EPS)
    return (y * g.astype(jnp.float32)).astype(x.dtype)


def head_rmsnorm(y, g):
    b, s, w = y.shape
    yf = y.astype(jnp.float32).reshape(b, s, w // HEAD_DIM, HEAD_DIM)
    yf = yf * lax.rsqrt(jnp.mean(yf * yf, axis=-1, keepdims=True) + EPS)
    return (yf.reshape(b, s, w) * g.astype(jnp.float32)).astype(y.dtype)


def short_gated_conv(b_gate, c_gate, u, w_conv):
    v = c_gate * u
    y = lax.conv_general_dilated(
        v, w_conv[:, None, :].astype(v.dtype),
        window_strides=(1,), padding=[(CONV_K - 1, 0)],
        dimension_numbers=('NWC', 'WIO', 'NWC'),
        feature_group_count=v.shape[-1])
    return b_gate * y


def stick_breaking_attention(q, k, v):
    b, h, s, dh = q.shape
    n_blk = s // Q_BLOCK
    scale = dh ** -0.5
    qb = q.reshape(b, h, n_blk, Q_BLOCK, dh).transpose(2, 0, 1, 3, 4)
    kf = k.astype(jnp.float32)
    vf = v.astype(jnp.float32)
    key_pos = jnp.arange(s)

    def one_block(args):
        qi, blk = args
        z = jnp.einsum('bhqd,bhkd->bhqk', qi.astype(jnp.float32), kf) * scale
        q_pos = blk * Q_BLOCK + jnp.arange(Q_BLOCK)
        mask = key_pos[None, :] < q_pos[:, None]
        log_keep = jnp.where(mask, jax.nn.log_sigmoid(-z), 0.0)
        suffix = lax.cumsum(log_keep, axis=3, reverse=True) - log_keep
        a = jnp.where(mask, jnp.exp(jax.nn.log_sigmoid(z) + suffix), 0.0)
        return jnp.einsum('bhqk,bhkd->bhqd', a, vf)

    out = lax.map(one_block, (qb, jnp.arange(n_blk)))
    return out.transpose(1, 2, 0, 3, 4).reshape(b, h, s, dh).astype(q.dtype)


def setup_inputs(seed: int = 0) -> dict:
    key = jax.random.key(seed)
    ks = jax.random.split(key, 20)
    f32 = jnp.float32

    def nrm(k, shape, fan_in):
        return jax.random.normal(k, shape, f32) * (fan_in ** -0.5)

    def gain(k, shape):
        return 1.0 + 0.02 * jax.random.normal(k, shape, f32)

    return {
        "x": jax.random.normal(ks[0], (BATCH, SEQ, D_MODEL), f32),
        "p": jax.random.normal(ks[1], (DEPTH, BATCH, SEQ, PLE_DIM), f32),
        "g_mix": gain(ks[2], (DEPTH, D_MODEL)),
        "w_in": nrm(ks[3], (DEPTH, D_MODEL, IN_COLS), D_MODEL),
        "conv_w": nrm(ks[4], (DEPTH, CONV_K, W_CONV), CONV_K),
        "g_conv_out": gain(ks[5], (DEPTH, W_CONV)),
        "g_attn_out": gain(ks[6], (DEPTH, W_ATTN)),
        "w_out": nrm(ks[7], (DEPTH, W_CONV + W_ATTN, D_MODEL), W_CONV + W_ATTN),
        "g_mlp": gain(ks[8], (DEPTH, D_MODEL)),
        "w_up": nrm(ks[9], (DEPTH, D_MODEL, D_FF), D_MODEL),
        "w_down": nrm(ks[10], (DEPTH, D_FF, D_MODEL), D_FF),
        "g_ple": gain(ks[11], (DEPTH, D_MODEL)),
        "w_ple_gate": nrm(ks[12], (DEPTH, D_MODEL, D_MODEL), D_MODEL),
        "w_ple_proj": nrm(ks[13], (DEPTH, PLE_DIM, D_MODEL), PLE_DIM),
        "g_final": gain(ks[14], (D_MODEL,)),
    }


def reference(x, p, g_mix, w_in, conv_w, g_conv_out, g_attn_out, w_out, g_mlp, w_up, w_down,
              g_ple, w_ple_gate, w_ple_proj, g_final):
    b, s, _ = x.shape
    splits = [W_CONV, 2 * W_CONV, 3 * W_CONV, 3 * W_CONV + W_ATTN, 3 * W_CONV + 2 * W_ATTN]
    h = x
    for i in range(DEPTH):
        a = rmsnorm(h, g_mix[i])
        proj = a @ w_in[i]
        cb, cc, cu, q, k, v = jnp.split(proj, splits, axis=-1)
        conv_out = head_rmsnorm(short_gated_conv(cb, cc, cu, conv_w[i]), g_conv_out[i])
        to_heads = lambda t: t.reshape(b, s, N_ATTN_HEADS, HEAD_DIM).transpose(0, 2, 1, 3)
        attn = stick_breaking_attention(to_heads(q), to_heads(k), to_heads(v))
        attn = head_rmsnorm(attn.transpose(0, 2, 1, 3).reshape(b, s, W_ATTN), g_attn_out[i])
        h = h + jnp.concatenate([conv_out, attn], axis=-1) @ w_out[i]
        m = rmsnorm(h, g_mlp[i])
        h = h + jnp.square(jax.nn.relu(m @ w_up[i])) @ w_down[i]
        gate = jax.nn.sigmoid(rmsnorm(h, g_ple[i]) @ w_ple_gate[i])
        h = h + gate * (p[i] @ w_ple_proj[i])
    return rmsnorm(h, g_final)
```

```python
import numpy as np
import ml_dtypes
from contextlib import ExitStack
import concourse.bass as bass
import concourse.mybir as mybir
from concourse.bass_utils import run_bass_kernel_spmd

F32 = mybir.dt.float32
BF16 = mybir.dt.bfloat16
AF = mybir.ActivationFunctionType
ALU = mybir.AluOpType

EPS = 1e-6
NEGV = -30000.0
KB = 1024
OWN = {0: [0, 3, 4, 7], 1: [1, 2, 5, 6]}
DEBUG = False


_USED = None
_WAITED = set()


class Eng:
    def __init__(self, nc, e, name, stack):
        self.e = e
        self.name = name
        self.sem = stack.enter_context(nc.semaphore("s_" + name))
        self.ord = 0
        self.cnt = 0
        self.val = {}
        self.seen = {}
        self.last = None

    def wait(self, *tks):
        for tk in tks:
            if tk is None:
                continue
            if isinstance(tk, list):
                self.wait(*tk)
                continue
            if tk[0] == "E":
                _, eng, o = tk
                if self.seen.get(eng.name, 0) >= o:
                    continue
                _WAITED.add((eng.name, o))
                self.e.wait_ge(eng.sem, eng.val[o])
                self.seen[eng.name] = o
            else:
                _, sem, v, key = tk
                if self.seen.get(key, 0) >= v:
                    continue
                self.e.wait_ge(sem, v)
                self.seen[key] = v

    def tick(self, ins):
        self.ord += 1
        if _USED is None or (self.name, self.ord) in _USED:
            ins.then_inc(self.sem, 1)
            self.cnt += 1
            self.val[self.ord] = self.cnt
        self.last = ("E", self, self.ord)
        return self.last


class DSem:
    def __init__(self, nc, name, stack):
        self.sem = stack.enter_context(nc.semaphore("d_" + name))
        self.n = 0
        self.key = "d_" + name

    def dma(self, q, out, in_):
        ins = q.e.dma_start(out=out, in_=in_)
        ins.then_inc(self.sem, 16)
        self.n += 16
        return ("D", self.sem, self.n, self.key)


def build_nc():
    global _USED, _WAITED
    _USED = None
    _WAITED = set()
    _build()
    _USED = set(_WAITED)
    _WAITED = set()
    nc = _build()
    assert _WAITED == _USED, (len(_WAITED), len(_USED))
    return nc


def _build():
    nc = bass.Bass("TRN2", target_bir_lowering=False)
    stack = ExitStack()

    def D(name, shape, kind="ExternalInput"):
        return nc.dram_tensor(name, shape, F32, kind=kind).ap()

    xa = D("xa", [4096, 1024])
    xo = D("xo", [2048, 1024])
    xh = D("xh", [8, 1024])
    po = D("po", [2048, 256])
    w_in = D("w_in", [1024, 3072])
    w_out = D("w_out", [1024, 1024])
    w_up = D("w_up", [1024, 4096])
    w_dn = D("w_dn", [4096, 1024])
    w_gate = D("w_gate", [1024, 1024])
    w_proj = D("w_proj", [256, 1024])
    cst = D("cst", [128, 512])
    gp = D("gp", [128, 20])
    gbc = D("gbc", [4, 128, 1024])
    neg = D("neg", [128, 16, 512])
    y = D("y", [2048, 1024], kind="ExternalOutput")
    dbg_out = {}

    arena = stack.enter_context(nc.sbuf_tensor("arena", [128, 212480 // 4], F32))
    ps = stack.enter_context(nc.psum_tensor("ps", [128, 4096], F32))

    def view(off, dtype, *shape):
        n = 1
        for s in shape:
            n *= s
        esz = 4 if dtype == F32 else 2
        nbytes = n * esz
        assert off % 4 == 0 and nbytes % 4 == 0
        ap = arena[:, off // 4:(off + nbytes) // 4]
        if dtype != F32:
            ap = ap.bitcast(dtype)
        if len(shape) == 2:
            ap = ap.rearrange("p (a b) -> p a b", a=shape[0])
        elif len(shape) == 3:
            ap = ap.rearrange("p (a b c) -> p a b c", a=shape[0], b=shape[1])
        return ap

    def bank(b):
        return ps[:, b * 512:(b + 1) * 512]

    pe = Eng(nc, nc.tensor, "pe", stack)
    act = Eng(nc, nc.scalar, "act", stack)
    vec = Eng(nc, nc.vector, "vec", stack)
    pool = Eng(nc, nc.gpsimd, "pool", stack)
    sp = Eng(nc, nc.sync, "sp", stack)
    for eng_ in (pe, act, vec, pool, sp):
        eng_.e.nop(cycle_cnt=12000)

    identf = view(0, F32, 128)
    cstb = view(512, BF16, 512)
    identb = cstb[:, 0:128]
    ntri = cstb[:, 128:256]
    nones = cstb[:, 256:384]
    bd = cstb[:, 384:512]
    gpt = view(1536, F32, 20)
    stat = view(1792, F32, 96)
    zl = view(2176, BF16, 64)
    aTh = view(2304, BF16, 8, 8)
    vhalo = view(2432, F32, 4, 8)
    cch = view(2560, F32, 4, 8)
    KT = view(4 * KB, BF16, 4, 4096)
    V = view(36 * KB, BF16, 32, 512)
    QT = view(68 * KB, BF16, 4, 2048)
    MIX = view(84 * KB, BF16, 8, 2048)
    W_IN = view(116 * KB, BF16, 8, 3072)
    XS = [view(164 * KB + s * 4 * KB, F32, 1024) for s in range(4)]
    AT = [view(180 * KB + s * 8 * KB, BF16, 8, 512) for s in range(2)]
    GB0 = view(196 * KB, F32, 1024)
    JUNK = view(200 * KB, BF16, 1024)
    cv = 100 * KB
    cb_sb = view(cv, F32, 512)
    cc_sb = view(cv + 2 * KB, F32, 512)
    cu_sb = view(cv + 4 * KB, F32, 512)
    vbuf = view(cv + 6 * KB, F32, 576)
    ybuf = view(cv + 6 * KB + 2304, F32, 512)
    ysq = view(cv + 6 * KB + 2304 + 2048, BF16, 512)
    tbuf = view(cv + 6 * KB + 2304 + 3072, F32, 512)
    rbuf = view(cv + 6 * KB + 2304 + 5120, F32, 512)
    NEG = view(116 * KB, BF16, 16, 512)
    LT = [view(132 * KB + s * KB, BF16, 512) for s in range(4)]
    ATT = [view(136 * KB + s * KB, BF16, 512) for s in range(4)]
    ET = [view(172 * KB + s * 2 * KB, F32, 512) for s in range(4)]
    PX = [view(180 * KB + s * 2 * KB, F32, 512) for s in range(2)]
    LSUM = [view(144 * KB + s * KB, BF16, 512) for s in range(2)]
    OSB = [view(146 * KB + s * 2 * KB, F32, 512) for s in range(2)]
    OSQ = [view(150 * KB + s * KB, BF16, 512) for s in range(2)]
    TB2 = [view(152 * KB + s * 2 * KB, F32, 512) for s in range(2)]
    W_OUT = view(156 * KB, BF16, 8, 1024)
    W_GATE = view(188 * KB, BF16, 8, 1024)
    H = [view(4 * KB + oc * 4 * KB, F32, 1024) for oc in range(16)]
    UT = [view(68 * KB + s * 8 * KB, BF16, 8, 512) for s in range(2)]
    MT = view(84 * KB, BF16, 8, 2048)
    UP = [view(116 * KB, BF16, 8, 1024), view(156 * KB, BF16, 8, 1024)]
    DN = [view(132 * KB, BF16, 8, 1024), view(172 * KB, BF16, 8, 1024)]
    W_PROJ = view(148 * KB, BF16, 2, 1024)
    GB1 = view(152 * KB, F32, 1024)
    A3 = [view(172 * KB + s * 4 * KB, F32, 1024) for s in range(2)]
    JUNK3 = view(180 * KB, BF16, 1024)
    RL = [view(204 * KB, F32, 512), view(152 * KB, F32, 512)]
    GB2 = view(116 * KB, F32, 1024)
    A5 = [view(120 * KB + s * 4 * KB, F32, 1024) for s in range(2)]
    GT = [view(128 * KB + s * 2 * KB, BF16, 8, 128) for s in range(2)]
    PS_ = [view(132 * KB + s * KB, F32, 256) for s in range(2)]
    PT = [view(134 * KB + s * 512, BF16, 2, 128) for s in range(2)]
    SG = [view(136 * KB + s * 2 * KB, F32, 512) for s in range(2)]
    OUTB = [view(156 * KB + s * 4 * KB, F32, 1024) for s in range(2)]
    JUNK5 = view(164 * KB, BF16, 1024)

    free = {}

    def fr(k):
        return free.get(k)

    d_c = DSem(nc, "c", stack)
    tk_c1 = d_c.dma(sp, identf, cst[:, 0:128])
    tk_c2 = d_c.dma(sp, gpt, gp[:, :])
    tk_c3 = d_c.dma(sp, GB0, gbc[0])
    d_cb = DSem(nc, "cb", stack)
    tk_cb = d_cb.dma(pool, cstb, cst[:, :])
    tk_zl = pool.tick(pool.e.memset(zl, 0.0))
    d_wkv = DSem(nc, "wkv", stack)
    d_win = DSem(nc, "win", stack)
    for kc in range(8):
        tk_wkv = d_wkv.dma(pool, W_IN[:, kc, 2048:3072], w_in[kc * 128:(kc + 1) * 128, 2048:3072])
    for c0 in (0, 1024):
        for kc in range(8):
            tk_win = d_win.dma(pool, W_IN[:, kc, c0:c0 + 1024], w_in[kc * 128:(kc + 1) * 128, c0:c0 + 1024])
    tk_consts = [tk_c3, tk_cb]

    stat_n = [0]

    def stat3():
        i = stat_n[0] % 30
        stat_n[0] += 1
        return stat[:, 3 * i:3 * i + 1], stat[:, 3 * i + 1:3 * i + 2], stat[:, 3 * i + 2:3 * i + 3]

    d_xs = [DSem(nc, "xs%d" % s, stack) for s in range(4)]
    xs_n = [0]
    tp_n = [0]

    def front(rows_ap, npart, gb, dst, junk, src_tile=None, load_tk=None, extra_dst_wait=None):
        s = xs_n[0] % 4
        xs_n[0] += 1
        xt = XS[s][0:npart, :]
        if src_tile is None:
            sp.wait(fr(("xs", s)))
            ld = d_xs[s].dma(sp, xt, rows_ap)
            src = xt
        else:
            ld = load_tk
            src = src_tile
        ssq, tt, rs = stat3()
        ssq, tt, rs = ssq[0:npart, :], tt[0:npart, :], rs[0:npart, :]
        act.wait(ld, tk_consts)
        a1 = act.tick(act.e.activation(out=junk[0:npart, :], in_=src, func=AF.Square, accum_out=ssq))
        act.wait(a1)
        a2 = act.tick(act.e.activation(out=tt, in_=ssq, func=AF.Ln, scale=1.0 / 1024.0, bias=EPS))
        act.wait(a2)
        a3 = act.tick(act.e.activation(out=rs, in_=tt, func=AF.Exp, scale=-0.5))
        vec.wait(a3, ld, tk_consts)
        if src_tile is not None:
            vec.wait(fr(("xs", s)))
        v1 = vec.tick(vec.e.scalar_tensor_tensor(out=xt, in0=src, scalar=rs, in1=gb[0:npart, :],
                                                 op0=ALU.mult, op1=ALU.mult))
        pr = tp_n[0] % 2
        tp_n[0] += 1
        tpb = ps[:, pr * 1024:(pr + 1) * 1024]
        pe.wait(v1, fr(("tp", pr)), tk_consts)
        for kc in range(8):
            ins = pe.e.transpose(out=tpb[:, kc * npart:(kc + 1) * npart], in_=xt[:, kc * 128:(kc + 1) * 128],
                                 identity=identf[0:npart, 0:npart])
        p1 = pe.tick(ins)
        free[("xs", s)] = p1
        act.wait(p1, extra_dst_wait)
        if npart == 128:
            act.tick(act.e.activation(out=dst[:, 0:4, :], in_=tpb[:, 0:512].rearrange("p (a b) -> p a b", a=4),
                                      func=AF.Copy))
            e1 = act.tick(act.e.activation(out=dst[:, 4:8, :],
                                           in_=tpb[:, 512:1024].rearrange("p (a b) -> p a b", a=4), func=AF.Copy))
        else:
            e1 = act.tick(act.e.activation(out=dst, in_=tpb[:, 0:8 * npart].rearrange("p (a b) -> p a b", a=8),
                                           func=AF.Copy))
        free[("tp", pr)] = e1
        return e1

    ring_n = [0]

    def ring_bank():
        b = 4 + ring_n[0] % 4
        ring_n[0] += 1
        return b

    def mm_group(b, pairs, cols=None):
        out = bank(b) if cols is None else bank(b)[:, cols[0]:cols[1]]
        pe.wait(fr(("bank", b)))
        n = len(pairs)
        for i, (l, r_) in enumerate(pairs):
            ins = pe.e.matmul(out, l, r_, start=(i == 0), stop=(i == n - 1))
        return pe.tick(ins)

    for j in range(8):
        buf = j % 2
        for cc in range(4):
            r0 = (j * 4 + cc) * 128
            e1 = front(xa[r0:r0 + 128, :], 128, GB0, AT[buf][:, :, cc * 128:(cc + 1) * 128], JUNK,
                       extra_dst_wait=fr(("at", buf)))
        pe.wait(e1, tk_wkv)
        for m in range(4):
            b = ring_bank()
            t = mm_group(b, [(W_IN[:, kc, 2048 + m * 128:2048 + (m + 1) * 128], AT[buf][:, kc, :]) for kc in range(8)])
            vec.wait(t)
            free[("bank", b)] = vec.tick(vec.e.tensor_copy(out=KT[:, m, j * 512:(j + 1) * 512], in_=bank(b)))
        for cc in range(4):
            b = ring_bank()
            t = mm_group(b, [(AT[buf][:, kc, cc * 128:(cc + 1) * 128], W_IN[:, kc, 2560:3072]) for kc in range(8)])
            vec.wait(t)
            free[("bank", b)] = vec.tick(vec.e.tensor_copy(out=V[:, j * 4 + cc, :], in_=bank(b)))
        free[("at", buf)] = t

    e1 = front(xh[:, :], 8, GB0, aTh, JUNK)
    pe.wait(e1, tk_win)
    hb = ring_bank()
    pe.wait(fr(("bank", hb)))
    for m8 in range(8):
        col0 = 512 + m8 * 128
        for kc in range(8):
            ins = pe.e.matmul(bank(hb)[:, m8 * 8:(m8 + 1) * 8], W_IN[:, kc, col0:col0 + 128], aTh[:, kc, :],
                              start=(kc == 0), stop=(kc == 7))
    t = pe.tick(ins)
    act.wait(t)
    a = act.tick(act.e.activation(out=cch, in_=bank(hb)[:, 0:32].rearrange("p (a b) -> p a b", a=4), func=AF.Copy))
    vec.wait(a)
    free[("bank", hb)] = vec.tick(vec.e.tensor_tensor(out=vhalo, in0=cch,
                                                      in1=bank(hb)[:, 32:64].rearrange("p (a b) -> p a b", a=4),
                                                      op=ALU.mult))
    tk_vhalo = free[("bank", hb)]

    for i in range(4):
        buf = i % 2
        for cc in range(4):
            r0 = (i * 4 + cc) * 128
            e1 = front(xo[r0:r0 + 128, :], 128, GB0, AT[buf][:, :, cc * 128:(cc + 1) * 128], JUNK,
                       extra_dst_wait=fr(("at", buf)))
        pe.wait(e1)
        tsl = slice(i * 512, (i + 1) * 512)
        for c in range(4):
            tks = []
            bs = []
            for col0 in (c * 128, 512 + c * 128, 1024 + c * 128):
                b = ring_bank()
                bs.append(b)
                tks.append(mm_group(b, [(W_IN[:, kc, col0:col0 + 128], AT[buf][:, kc, :]) for kc in range(8)]))
            act.wait(tks[0], fr("cb_sb"))
            free[("bank", bs[0])] = a_cb = act.tick(act.e.activation(out=cb_sb, in_=bank(bs[0]), func=AF.Copy))
            act.wait(tks[1], fr("cc_sb"))
            free[("bank", bs[1])] = a_cc = act.tick(act.e.activation(out=cc_sb, in_=bank(bs[1]), func=AF.Copy))
            act.wait(tks[2], fr("cu_sb"))
            free[("bank", bs[2])] = a_cu = act.tick(act.e.activation(out=cu_sb, in_=bank(bs[2]), func=AF.Copy))
            pool.wait(a_cc, a_cu, fr("vbuf"), tk_vhalo)
            pool.tick(pool.e.tensor_copy(out=vbuf[:, 0:2], in_=vhalo[:, c, 2 * i:2 * i + 2]))
            g1 = pool.tick(pool.e.tensor_tensor(out=vbuf[:, 2:514], in0=cc_sb, in1=cu_sb, op=ALU.mult))
            free["cc_sb"] = g1
            free["cu_sb"] = g1
            w0 = gpt[:, 8 + c * 3:9 + c * 3]
            w1 = gpt[:, 9 + c * 3:10 + c * 3]
            w2 = gpt[:, 10 + c * 3:11 + c * 3]
            vec.wait(g1, fr("ybuf"), tk_consts)
            d1 = vec.tick(vec.e.tensor_scalar(out=ybuf, in0=vbuf[:, 0:512], scalar1=w0, scalar2=None, op0=ALU.mult))
            vec.wait(d1)
            d2 = vec.tick(vec.e.scalar_tensor_tensor(out=ybuf, in0=vbuf[:, 1:513], scalar=w1, in1=ybuf,
                                                     op0=ALU.mult, op1=ALU.add))
            vec.wait(d2)
            d3 = vec.tick(vec.e.scalar_tensor_tensor(out=ybuf, in0=vbuf[:, 2:514], scalar=w2, in1=ybuf,
                                                     op0=ALU.mult, op1=ALU.add))
            free["vbuf"] = d3
            vec.wait(d3, a_cb)
            d4 = vec.tick(vec.e.tensor_tensor(out=cb_sb, in0=ybuf, in1=cb_sb, op=ALU.mult))
            free["ybuf"] = d4
            act.wait(d4, fr("ysq"))
            a_sq = act.tick(act.e.activation(out=ysq, in_=cb_sb, func=AF.Square))
            b = ring_bank()
            pe.wait(a_sq, tk_consts)
            t_ms = mm_group(b, [(bd, ysq)])
            free["ysq"] = t_ms
            act.wait(t_ms, fr("tbuf"))
            a_ln = act.tick(act.e.activation(out=tbuf, in_=bank(b), func=AF.Ln, bias=EPS))
            free[("bank", b)] = a_ln
            act.wait(a_ln, fr("rbuf"))
            a_rs = act.tick(act.e.activation(out=rbuf, in_=tbuf, func=AF.Exp, scale=-0.5))
            free["tbuf"] = a_rs
            vec.wait(a_rs)
            d5 = vec.tick(vec.e.scalar_tensor_tensor(out=MIX[:, c, tsl], in0=cb_sb, scalar=gpt[:, c:c + 1], in1=rbuf,
                                                     op0=ALU.mult, op1=ALU.mult))
            free["cb_sb"] = d5
            free["rbuf"] = d5
        for m in range(4):
            b = ring_bank()
            t = mm_group(b, [(W_IN[:, kc, 1536 + m * 128:1536 + (m + 1) * 128], AT[buf][:, kc, :]) for kc in range(8)])
            vec.wait(t)
            free[("bank", b)] = vec.tick(vec.e.tensor_scalar(out=QT[:, m, tsl], in0=bank(b), scalar1=0.125,
                                                             scalar2=None, op0=ALU.mult))
        free[("at", buf)] = t

    p1_done = [pe.last, act.last, vec.last, pool.last]

    d_neg = DSem(nc, "neg", stack)
    d_wo = DSem(nc, "wo", stack)
    d_wg = DSem(nc, "wg", stack)
    pool.wait(p1_done)
    for g in range(4):
        tk_neg = d_neg.dma(pool, NEG[:, g * 4:(g + 1) * 4, :], neg[:, g * 4:(g + 1) * 4, :])
    for kc in range(8):
        tk_wo = d_wo.dma(pool, W_OUT[:, kc, :], w_out[kc * 128:(kc + 1) * 128, :])
    for kc in range(8):
        tk_wg = d_wg.dma(pool, W_GATE[:, kc, :], w_gate[kc * 128:(kc + 1) * 128, :])

    units = []
    for i in range(4):
        ntile = 2 * i + 2
        for h in range(8):
            ul = []
            for tix in range(ntile - 1, -1, -1):
                for b in range(3, -1, -1):
                    top = (tix == ntile - 1)
                    second = (tix == ntile - 2)
                    c0 = 128 * b if top else 0
                    mk = None
                    if top:
                        mk = NEG[:, (i % 2) * 8 + b, :]
                    elif second:
                        mk = NEG[:, (i % 2) * 8 + 4 + b, :]
                    ul.append(dict(i=i, h=h, kb=tix * 4 + b, c0=c0, mk=mk))
            for u, d in enumerate(ul):
                d["u"] = u
                d["first"] = (u == 0)
                d["last"] = (u == len(ul) - 1)
            units.extend(ul)
    for k, d in enumerate(units):
        d["k"] = k

    SB = [0, 1, 2]
    SB2 = [6, 7]
    OB = [3, 4]
    MSB = 5
    deferred = []

    def st_z(d):
        k, i, h, c0 = d["k"], d["i"], d["h"], d["c0"]
        hc, hp = h // 2, (h % 2) * 64
        sb = SB[k % 3]
        out = bank(sb)[:, c0:512]
        pe.wait(fr(("bank", sb)), p1_done)
        kk = d["kb"] * 128
        has_mk = d["mk"] is not None
        ins = pe.e.matmul(out, KT[hp:hp + 64, hc, kk:kk + 128], QT[hp:hp + 64, hc, i * 512 + c0:(i + 1) * 512],
                          start=True, stop=not has_mk)
        if has_mk:
            pe.wait(tk_neg)
            ins = pe.e.matmul(out, identb, d["mk"][:, c0:512], start=False, stop=True)
        d["t_z"] = pe.tick(ins)

    def st_e1(d):
        k, c0 = d["k"], d["c0"]
        sb = SB[k % 3]
        act.wait(d["t_z"], fr(("et", k % 4)))
        d["t_e1"] = act.tick(act.e.activation(out=ET[k % 4][:, c0:512], in_=bank(sb)[:, c0:512], func=AF.Exp))
        free[("bank", sb)] = d["t_e1"]

    def st_ln(d):
        k, c0 = d["k"], d["c0"]
        act.wait(d["t_e1"], fr(("lt", k % 4)))
        d["t_ln"] = act.tick(act.e.activation(out=LT[k % 4][:, c0:512], in_=ET[k % 4][:, c0:512], func=AF.Ln, bias=1.0))

    def st_tc(d):
        k, c0, h = d["k"], d["c0"], d["h"]
        sb = SB2[k % 2]
        out = bank(sb)[:, c0:512]
        pe.wait(d["t_ln"], tk_consts, fr(("bank", sb)))
        ins = pe.e.matmul(out, ntri, LT[k % 4][:, c0:512], start=True, stop=d["first"])
        if not d["first"]:
            pe.wait(fr(("lsum_w", h % 2)))
            ins = pe.e.matmul(out, nones, LSUM[h % 2][:, c0:512], start=False, stop=True)
        d["t_tc"] = pe.tick(ins)

    def st_add(d):
        k, c0, h = d["k"], d["c0"], d["h"]
        if d["first"]:
            pool.wait(fr(("lsum_r", h % 2)))
            free[("lsum_w", h % 2)] = pool.tick(pool.e.memset(LSUM[h % 2], 0.0))
        if d["last"]:
            free[("lt", k % 4)] = d["t_tc"]
            free[("lsum_r", h % 2)] = d["t_tc"]
            return
        pool.wait(d["t_ln"], d["t_tc"], fr(("lsum_w", h % 2)))
        t = pool.tick(pool.e.tensor_tensor(out=LSUM[h % 2][:, c0:512], in0=LSUM[h % 2][:, c0:512],
                                           in1=LT[k % 4][:, c0:512], op=ALU.add))
        free[("lsum_w", h % 2)] = t
        free[("lt", k % 4)] = [t, d["t_tc"]]

    def st_e3(d):
        k, c0 = d["k"], d["c0"]
        sb = SB2[k % 2]
        act.wait(d["t_tc"], fr(("px", k % 2)))
        d["t_e3"] = act.tick(act.e.activation(out=PX[k % 2][:, c0:512], in_=bank(sb)[:, c0:512], func=AF.Exp))
        free[("bank", sb)] = d["t_e3"]
        vec.wait(d["t_e3"], d["t_e1"], fr(("att", k % 4)))
        d["t_m"] = vec.tick(vec.e.tensor_tensor(out=ATT[k % 4][:, c0:512], in0=ET[k % 4][:, c0:512],
                                                in1=PX[k % 2][:, c0:512], op=ALU.mult))
        free[("px", k % 2)] = d["t_m"]
        free[("et", k % 4)] = [d["t_m"], d["t_ln"]]

    def st_av(d):
        k, c0, h, i = d["k"], d["c0"], d["h"], d["i"]
        pair = (i * 8 + h) // 2
        ob = OB[pair % 2]
        hp = (h % 2) * 64
        if d["first"]:
            if h % 2 == 0:
                pe.wait(fr(("bank", ob)))
            pe.wait(tk_zl, tk_consts)
            pe.e.matmul(bank(ob)[hp:hp + 64, :], zl, cstb, start=True, stop=False)
        pe.wait(d["t_m"])
        ins = pe.e.matmul(bank(ob)[hp:hp + 64, c0:512], V[:, d["kb"], h * 64:(h + 1) * 64], ATT[k % 4][:, c0:512],
                          start=False, stop=d["last"])
        t = pe.tick(ins)
        free[("att", k % 4)] = t
        if d["last"] and h % 2 == 1:
            fin_pair(i, h // 2, pair, ob, t)

    def fin_pair(i, hpair, pair, ob, t_av):
        s2 = pair % 2
        tsl = slice(i * 512, (i + 1) * 512)

        def f1():
            vec.wait(t_av, fr(("osb", s2)))
            c1 = vec.tick(vec.e.tensor_copy(out=OSB[s2], in_=bank(ob)))
            free[("bank", ob)] = c1
            pool.wait(c1, fr(("osq", s2)))
            q1 = pool.tick(pool.e.tensor_tensor(out=OSQ[s2], in0=OSB[s2], in1=OSB[s2], op=ALU.mult))

            def f2():
                pe.wait(q1, fr(("bank", MSB)))
                ins = pe.e.matmul(bank(MSB), bd, OSQ[s2], start=True, stop=True)
                t_ms = pe.tick(ins)
                free[("osq", s2)] = t_ms

                def f3():
                    act.wait(t_ms, fr(("tb2", s2)))
                    a_ln = act.tick(act.e.activation(out=TB2[s2], in_=bank(MSB), func=AF.Ln, bias=EPS))
                    free[("bank", MSB)] = a_ln
                    act.wait(a_ln)
                    a_rs = act.tick(act.e.activation(out=TB2[s2], in_=TB2[s2], func=AF.Exp, scale=-0.5))
                    vec.wait(a_rs)
                    d5 = vec.tick(vec.e.scalar_tensor_tensor(out=MIX[:, 4 + hpair, tsl], in0=OSB[s2],
                                                             scalar=gpt[:, 4 + hpair:5 + hpair], in1=TB2[s2],
                                                             op0=ALU.mult, op1=ALU.mult))
                    free[("osb", s2)] = d5
                    free[("tb2", s2)] = d5
                deferred.append([1, f3])
            deferred.append([1, f2])
        deferred.append([0, f1])

    def run_deferred(flush=False):
        while True:
            ready = [x for x in deferred if x[0] <= 0]
            if not ready:
                if flush and deferred:
                    for x in deferred:
                        x[0] -= 1
                    continue
                break
            for x in ready:
                deferred.remove(x)
                x[1]()
        for x in deferred:
            x[0] -= 1

    NU = len(units)
    st_z(units[0])
    for k in range(NU + 1):
        if k + 1 < NU:
            st_z(units[k + 1])
        if k < NU:
            st_e1(units[k])
        if k >= 1:
            st_e3(units[k - 1])
        if k < NU:
            st_ln(units[k])
            st_tc(units[k])
            st_add(units[k])
        if k >= 1:
            st_av(units[k - 1])
        run_deferred()
    run_deferred(flush=True)

    p2_done = [pe.last, act.last, vec.last, pool.last]

    d_h = [DSem(nc, "h%d" % s, stack) for s in range(16)]
    d_wp = DSem(nc, "wp", stack)
    d_g1 = DSem(nc, "g1", stack)
    d_ml = [DSem(nc, "ml%d" % s, stack) for s in range(2)]

    def load_quarter(qq, bufq, waits):
        pool.wait(waits)
        for kc in range(8):
            d_ml[bufq].dma(pool, UP[bufq][:, kc, :], w_up[kc * 128:(kc + 1) * 128, qq * 1024:(qq + 1) * 1024])
        for fc in range(8):
            tk = d_ml[bufq].dma(pool, DN[bufq][:, fc, :], w_dn[qq * 1024 + fc * 128:qq * 1024 + (fc + 1) * 128, :])
        return tk

    pool.wait(p2_done)
    for kc in range(2):
        tk_wp = d_wp.dma(pool, W_PROJ[:, kc, :], w_proj[kc * 128:(kc + 1) * 128, :])
    tk_q = [None] * 4
    tk_q[0] = load_quarter(0, 0, None)
    sp.wait(p2_done)
    tk_g1 = d_g1.dma(sp, GB1, gbc[1])
    tk_h = []
    for oc in range(16):
        tk_h.append(d_h[oc].dma(sp, H[oc], xo[oc * 128:(oc + 1) * 128, :]))
    t_h1 = [None] * 16
    for oc in range(16):
        for hf in range(2):
            b = ring_bank()
            pe.wait(tk_wo, p2_done)
            t = mm_group(b, [(MIX[:, kc, oc * 128:(oc + 1) * 128], W_OUT[:, kc, hf * 512:(hf + 1) * 512])
                             for kc in range(8)])
            vec.wait(t, tk_h[oc])
            hs = H[oc][:, hf * 512:(hf + 1) * 512]
            t_h1[oc] = free[("bank", b)] = vec.tick(vec.e.tensor_tensor(out=hs, in0=hs, in1=bank(b), op=ALU.add))
    p3a_pe = pe.last

    a3_n = [0]

    def front3(src, gb, dst, junk, abufs, src_tk, dst_wait, gb_tk):
        s = a3_n[0] % 2
        a3_n[0] += 1
        xt = abufs[s]
        ssq, tt, rs = stat3()
        act.wait(src_tk)
        a1 = act.tick(act.e.activation(out=junk, in_=src, func=AF.Square, accum_out=ssq))
        act.wait(a1)
        a2 = act.tick(act.e.activation(out=tt, in_=ssq, func=AF.Ln, scale=1.0 / 1024.0, bias=EPS))
        act.wait(a2)
        a3 = act.tick(act.e.activation(out=rs, in_=tt, func=AF.Exp, scale=-0.5))
        vec.wait(a3, src_tk, fr(("a3", id(abufs), s)), gb_tk)
        v1 = vec.tick(vec.e.scalar_tensor_tensor(out=xt, in0=src, scalar=rs, in1=gb, op0=ALU.mult, op1=ALU.mult))
        pr = tp_n[0] % 2
        tp_n[0] += 1
        tpb = ps[:, pr * 1024:(pr + 1) * 1024]
        pe.wait(v1, fr(("tp", pr)))
        for kc in range(8):
            ins = pe.e.transpose(out=tpb[:, kc * 128:(kc + 1) * 128], in_=xt[:, kc * 128:(kc + 1) * 128],
                                 identity=identf)
        p1 = pe.tick(ins)
        free[("a3", id(abufs), s)] = p1
        act.wait(p1, dst_wait)
        act.tick(act.e.activation(out=dst[:, 0:4, :], in_=tpb[:, 0:512].rearrange("p (a b) -> p a b", a=4),
                                  func=AF.Copy))
        e1 = act.tick(act.e.activation(out=dst[:, 4:8, :], in_=tpb[:, 512:1024].rearrange("p (a b) -> p a b", a=4),
                                       func=AF.Copy))
        free[("tp", pr)] = e1
        return e1

    for oc in range(16):
        t_mt = front3(H[oc], GB1, MT[:, :, oc * 128:(oc + 1) * 128], JUNK3, A3, t_h1[oc], p3a_pe, tk_g1)
    mt_done = [pe.last, act.last, vec.last]

    ut_n = [0]
    rl_n = [0]
    for qq in range(4):
        bq = qq % 2
        if qq + 1 < 4:
            tk_q[qq + 1] = load_quarter(qq + 1, 1 - bq, mt_done if qq == 0 else fr(("mlq", 1 - bq)))
        for i in range(4):
            ub = ut_n[0] % 2
            ut_n[0] += 1
            pe.wait(tk_q[qq], t_mt)
            for fc in range(8):
                b = ring_bank()
                t = mm_group(b, [(UP[bq][:, kc, fc * 128:(fc + 1) * 128], MT[:, kc, i * 512:(i + 1) * 512])
                                 for kc in range(8)])
                rs_ = rl_n[0] % 2
                rl_n[0] += 1
                act.wait(t, fr(("rl", rs_)))
                a_r = act.tick(act.e.activation(out=RL[rs_], in_=bank(b), func=AF.Relu))
                free[("bank", b)] = a_r
                pool.wait(a_r, fr(("ut", ub)) if fc == 0 else None)
                g_ = pool.tick(pool.e.tensor_tensor(out=UT[ub][:, fc, :], in0=RL[rs_], in1=RL[rs_], op=ALU.mult))
                free[("rl", rs_)] = g_
            pe.wait(g_)
            for cc in range(4):
                oc = i * 4 + cc
                for hf in range(2):
                    b = ring_bank()
                    t = mm_group(b, [(UT[ub][:, fc, cc * 128:(cc + 1) * 128], DN[bq][:, fc, hf * 512:(hf + 1) * 512])
                                     for fc in range(8)])
                    vec.wait(t, t_h1[oc])
                    hs = H[oc][:, hf * 512:(hf + 1) * 512]
                    t_h1[oc] = free[("bank", b)] = vec.tick(vec.e.tensor_tensor(out=hs, in0=hs, in1=bank(b), op=ALU.add))
            free[("ut", ub)] = t
        free[("mlq", bq)] = pe.last
    p3b_done = [pe.last, act.last, vec.last, pool.last]

    d_g2 = DSem(nc, "g2", stack)
    d_g3 = DSem(nc, "g3", stack)
    d_p = [DSem(nc, "p%d" % s, stack) for s in range(2)]
    d_o = [DSem(nc, "o%d" % s, stack) for s in range(2)]
    sp.wait(p3b_done)
    act.wait(p3b_done)
    vec.wait(p3b_done)
    pe.wait(p3b_done)
    tk_g2 = d_g2.dma(sp, GB2, gbc[2])
    tk_g3 = d_g3.dma(sp, GB1, gbc[3])
    o_tk = [None, None]
    for oc in range(16):
        s2 = oc % 2
        e1 = front3(H[oc], GB2, GT[s2], JUNK5, A5, t_h1[oc], fr(("gt", s2)), tk_g2)
        sp.wait(fr(("ps", s2)))
        tk_p = d_p[s2].dma(sp, PS_[s2], po[oc * 128:(oc + 1) * 128, :])
        b = ring_bank()
        pe.wait(tk_p, fr(("bank", b)))
        for k2 in range(2):
            ins = pe.e.transpose(out=bank(b)[:, k2 * 128:(k2 + 1) * 128], in_=PS_[s2][:, k2 * 128:(k2 + 1) * 128],
                                 identity=identf)
        t = pe.tick(ins)
        free[("ps", s2)] = t
        vec.wait(t, fr(("pt", s2)))
        free[("bank", b)] = c_pt = vec.tick(vec.e.tensor_copy(out=PT[s2],
                                                             in_=bank(b)[:, 0:256].rearrange("p (a b) -> p a b", a=2)))
        pe.wait(e1, c_pt, tk_wg, tk_wp)
        for hf in range(2):
            bg = ring_bank()
            t_g = mm_group(bg, [(GT[s2][:, kc, :], W_GATE[:, kc, hf * 512:(hf + 1) * 512]) for kc in range(8)])
            bp = ring_bank()
            t_p = mm_group(bp, [(PT[s2][:, k2, :], W_PROJ[:, k2, hf * 512:(hf + 1) * 512]) for k2 in range(2)])
            sg = SG[hf]
            act.wait(t_g, fr(("sg", hf)))
            a_e = act.tick(act.e.activation(out=sg, in_=bank(bg), func=AF.Exp, scale=-1.0))
            free[("bank", bg)] = a_e
            vec.wait(a_e)
            v1 = vec.tick(vec.e.tensor_scalar(out=sg, in0=sg, scalar1=1.0, scalar2=None, op0=ALU.add))
            vec.wait(v1)
            v2 = vec.tick(vec.e.reciprocal(out=sg, in_=sg))
            vec.wait(v2, t_p)
            v3 = vec.tick(vec.e.tensor_tensor(out=sg, in0=sg, in1=bank(bp), op=ALU.mult))
            free[("bank", bp)] = v3
            hs = H[oc][:, hf * 512:(hf + 1) * 512]
            vec.wait(v3, t_h1[oc])
            v4 = vec.tick(vec.e.tensor_tensor(out=hs, in0=hs, in1=sg, op=ALU.add))
            free[("sg", hf)] = v4
        free[("gt", s2)] = t_g
        free[("pt", s2)] = t_p
        ssq, tt, rs = stat3()
        act.wait(v4)
        a1 = act.tick(act.e.activation(out=JUNK5, in_=H[oc], func=AF.Square, accum_out=ssq))
        act.wait(a1)
        a2 = act.tick(act.e.activation(out=tt, in_=ssq, func=AF.Ln, scale=1.0 / 1024.0, bias=EPS))
        act.wait(a2)
        a3 = act.tick(act.e.activation(out=rs, in_=tt, func=AF.Exp, scale=-0.5))
        vec.wait(a3, o_tk[s2], tk_g3)
        v5 = vec.tick(vec.e.scalar_tensor_tensor(out=OUTB[s2], in0=H[oc], scalar=rs, in1=GB1,
                                                 op0=ALU.mult, op1=ALU.mult))
        sp.wait(v5)
        o_tk[s2] = d_o[s2].dma(sp, y[oc * 128:(oc + 1) * 128, :], OUTB[s2])
    sp.wait(o_tk[0], o_tk[1])
    stack.close()
    return nc


def _consts():
    c = np.zeros((128, 512), np.float32)
    j = np.arange(128)[:, None]
    s = np.arange(128)[None, :]
    c[:, 0:128] = (j == s)
    c[:, 128:256] = np.where(j >= s, -1.0, 0.0)
    c[:, 256:384] = -1.0
    c[:, 384:512] = np.where((j // 64) == (s // 64), 1.0 / 64.0, 0.0)
    return c


def _neg(r):
    out = np.zeros((128, 16, 512), np.float32)
    s = np.arange(128)[:, None]
    t = np.arange(512)[None, :]
    for e in range(2):
        typ = "A" if (e + r) % 2 == 1 else "B"
        for b in range(4):
            tri = np.where((128 * b + s) < t, 0.0, NEGV).astype(np.float32)
            if typ == "A":
                out[:, e * 8 + b, :] = tri
                out[:, e * 8 + 4 + b, :] = 0.0
            else:
                out[:, e * 8 + b, :] = NEGV
                out[:, e * 8 + 4 + b, :] = tri
    return out


_NC_CACHE = {}


def kernel(x, p, g_mix, w_in, conv_w, g_conv_out, g_attn_out, w_out, g_mlp, w_up, w_down,
           g_ple, w_ple_gate, w_ple_proj, g_final):
    f = lambda a: np.ascontiguousarray(np.asarray(a, dtype=np.float32))
    x, p = f(x), f(p)
    g_mix, g_mlp, g_ple, g_final = f(g_mix), f(g_mlp), f(g_ple), f(g_final)
    conv_w, g_conv_out, g_attn_out = f(conv_w), f(g_conv_out), f(g_attn_out)
    wi, wo, wu, wd, wg, wp = f(w_in)[0], f(w_out)[0], f(w_up)[0], f(w_down)[0], f(w_ple_gate)[0], f(w_ple_proj)[0]

    gp = np.zeros((128, 20), np.float32)
    gp[:, 0:4] = g_conv_out[0].reshape(4, 128).T
    gp[:, 4:8] = g_attn_out[0].reshape(4, 128).T
    for c in range(4):
        for k in range(3):
            gp[:, 8 + c * 3 + k] = conv_w[0, k, c * 128:(c + 1) * 128]
    gbc = np.stack([np.broadcast_to(v, (128, 1024)) for v in (g_mix[0], g_mlp[0], g_ple[0], g_final)]).astype(np.float32)
    gbc = np.ascontiguousarray(gbc)
    cst = _consts()

    in_maps = []
    for c in range(8):
        b, r = c // 2, c % 2
        tiles = OWN[r]
        xo = np.concatenate([x[b, t * 512:(t + 1) * 512] for t in tiles], axis=0)
        xh = np.zeros((8, 1024), np.float32)
        for i, t in enumerate(tiles):
            if t > 0:
                xh[2 * i:2 * i + 2] = x[b, t * 512 - 2:t * 512]
        po = np.concatenate([p[0, b, t * 512:(t + 1) * 512] for t in tiles], axis=0)
        in_maps.append({
            "xa": x[b], "xo": np.ascontiguousarray(xo), "xh": xh, "po": np.ascontiguousarray(po),
            "w_in": wi, "w_out": wo, "w_up": wu, "w_dn": wd, "w_gate": wg, "w_proj": wp,
            "cst": cst, "gp": gp, "gbc": gbc, "neg": _neg(r),
        })
    if "nc" not in _NC_CACHE:
        _NC_CACHE["nc"] = build_nc()
    nc = _NC_CACHE["nc"]
    res = run_bass_kernel_spmd(nc, in_maps, core_ids=list(range(8)))
    out = np.zeros((4, 4096, 1024), np.float32)
    for c in range(8):
        b, r = c // 2, c % 2
        yc = np.asarray(res.results[c]["y"], dtype=np.float32)
        for i, t in enumerate(OWN[r]):
            out[b, t * 512:(t + 1) * 512] = yc[i * 512:(i + 1) * 512]
    return out
```

```python
import numpy as np
import ml_dtypes
from contextlib import ExitStack
import concourse.bass as bass
import concourse.mybir as mybir
from concourse.bass_utils import run_bass_kernel_spmd

F32 = mybir.dt.float32
BF16 = mybir.dt.bfloat16
AF = mybir.ActivationFunctionType
ALU = mybir.AluOpType

EPS = 1e-6
NEGV = -30000.0
KB = 1024
OWN = {0: [0, 3, 4, 7], 1: [1, 2, 5, 6]}
DEBUG = False


_USED = None
_WAITED = set()


class Eng:
    def __init__(self, nc, e, name, stack):
        self.e = e
        self.name = name
        self.sem = stack.enter_context(nc.semaphore("s_" + name))
        self.ord = 0
        self.cnt = 0
        self.val = {}
        self.seen = {}
        self.last = None

    def wait(self, *tks):
        for tk in tks:
            if tk is None:
                continue
            if isinstance(tk, list):
                self.wait(*tk)
                continue
            if tk[0] == "E":
                _, eng, o = tk
                if self.seen.get(eng.name, 0) >= o:
                    continue
                _WAITED.add((eng.name, o))
                self.e.wait_ge(eng.sem, eng.val[o])
                self.seen[eng.name] = o
            else:
                _, sem, v, key = tk
                if self.seen.get(key, 0) >= v:
                    continue
                self.e.wait_ge(sem, v)
                self.seen[key] = v

    def tick(self, ins):
        self.ord += 1
        if _USED is None or (self.name, self.ord) in _USED:
            ins.then_inc(self.sem, 1)
            self.cnt += 1
            self.val[self.ord] = self.cnt
        self.last = ("E", self, self.ord)
        return self.last


class DSem:
    def __init__(self, nc, name, stack):
        self.sem = stack.enter_context(nc.semaphore("d_" + name))
        self.n = 0
        self.key = "d_" + name

    def dma(self, q, out, in_):
        ins = q.e.dma_start(out=out, in_=in_)
        ins.then_inc(self.sem, 16)
        self.n += 16
        return ("D", self.sem, self.n, self.key)


def build_nc():
    global _USED, _WAITED
    _USED = None
    _WAITED = set()
    _build()
    _USED = set(_WAITED)
    _WAITED = set()
    nc = _build()
    assert _WAITED == _USED, (len(_WAITED), len(_USED))
    return nc


def _build():
    nc = bass.Bass("TRN2", target_bir_lowering=False)
    stack = ExitStack()

    def D(name, shape, kind="ExternalInput"):
        return nc.dram_tensor(name, shape, F32, kind=kind).ap()

    xa = D("xa", [4096, 1024])
    xo = D("xo", [2048, 1024])
    xh = D("xh", [8, 1024])
    po = D("po", [2048, 256])
    w_in = D("w_in", [1024, 3072])
    w_out = D("w_out", [1024, 1024])
    w_up = D("w_up", [1024, 4096])
    w_dn = D("w_dn", [4096, 1024])
    w_gate = D("w_gate", [1024, 1024])
    w_proj = D("w_proj", [256, 1024])
    cst = D("cst", [128, 512])
    gp = D("gp", [128, 20])
    gbc = D("gbc", [4, 128, 1024])
    neg = D("neg", [128, 16, 512])
    y = D("y", [2048, 1024], kind="ExternalOutput")
    dbg_out = {}

    arena = stack.enter_context(nc.sbuf_tensor("arena", [128, 212480 // 4], F32))
    ps = stack.enter_context(nc.psum_tensor("ps", [128, 4096], F32))

    def view(off, dtype, *shape):
        n = 1
        for s in shape:
            n *= s
        esz = 4 if dtype == F32 else 2
        nbytes = n * esz
        assert off % 4 == 0 and nbytes % 4 == 0
        ap = arena[:, off // 4:(off + nbytes) // 4]
        if dtype != F32:
            ap = ap.bitcast(dtype)
        if len(shape) == 2:
            ap = ap.rearrange("p (a b) -> p a b", a=shape[0])
        elif len(shape) == 3:
            ap = ap.rearrange("p (a b c) -> p a b c", a=shape[0], b=shape[1])
        return ap

    def bank(b):
        return ps[:, b * 512:(b + 1) * 512]

    pe = Eng(nc, nc.tensor, "pe", stack)
    act = Eng(nc, nc.scalar, "act", stack)
    vec = Eng(nc, nc.vector, "vec", stack)
    pool = Eng(nc, nc.gpsimd, "pool", stack)
    sp = Eng(nc, nc.sync, "sp", stack)
    for eng_ in (pe, act, vec, pool, sp):
        eng_.e.nop(cycle_cnt=12000)

    identf = view(0, F32, 128)
    cstb = view(512, BF16, 512)
    identb = cstb[:, 0:128]
    ntri = cstb[:, 128:256]
    nones = cstb[:, 256:384]
    bd = cstb[:, 384:512]
    gpt = view(1536, F32, 20)
    stat = view(1792, F32, 96)
    zl = view(2176, BF16, 64)
    aTh = view(2304, BF16, 8, 8)
    vhalo = view(2432, F32, 4, 8)
    cch = view(2560, F32, 4, 8)
    KT = view(4 * KB, BF16, 4, 4096)
    V = view(36 * KB, BF16, 32, 512)
    QT = view(68 * KB, BF16, 4, 2048)
    MIX = view(84 * KB, BF16, 8, 2048)
    W_IN = view(116 * KB, BF16, 8, 3072)
    XS = [view(164 * KB + s * 4 * KB, F32, 1024) for s in range(4)]
    AT = [view(180 * KB + s * 8 * KB, BF16, 8, 512) for s in range(2)]
    GB0 = view(196 * KB, F32, 1024)
    JUNK = view(200 * KB, BF16, 1024)
    cv = 100 * KB
    cb_sb = view(cv, F32, 512)
    cc_sb = view(cv + 2 * KB, F32, 512)
    cu_sb = view(cv + 4 * KB, F32, 512)
    vbuf = view(cv + 6 * KB, F32, 576)
    ybuf = view(cv + 6 * KB + 2304, F32, 512)
    ysq = view(cv + 6 * KB + 2304 + 2048, BF16, 512)
    tbuf = view(cv + 6 * KB + 2304 + 3072, F32, 512)
    rbuf = view(cv + 6 * KB + 2304 + 5120, F32, 512)
    NEG = view(116 * KB, BF16, 16, 512)
    LT = [view(132 * KB + s * KB, BF16, 512) for s in range(4)]
    ATT = [view(136 * KB + s * KB, BF16, 512) for s in range(4)]
    ET = [view(172 * KB + s * 2 * KB, F32, 512) for s in range(4)]
    PX = [view(180 * KB + s * 2 * KB, F32, 512) for s in range(2)]
    LSUM = [view(144 * KB + s * KB, BF16, 512) for s in range(2)]
    OSB = [view(146 * KB + s * 2 * KB, F32, 512) for s in range(2)]
    OSQ = [view(150 * KB + s * KB, BF16, 512) for s in range(2)]
    TB2 = [view(152 * KB + s * 2 * KB, F32, 512) for s in range(2)]
    W_OUT = view(156 * KB, BF16, 8, 1024)
    W_GATE = view(188 * KB, BF16, 8, 1024)
    H = [view(4 * KB + oc * 4 * KB, F32, 1024) for oc in range(16)]
    UT = [view(68 * KB + s * 8 * KB, BF16, 8, 512) for s in range(2)]
    MT = view(84 * KB, BF16, 8, 2048)
    UP = [view(116 * KB, BF16, 8, 1024), view(156 * KB, BF16, 8, 1024)]
    DN = [view(132 * KB, BF16, 8, 1024), view(172 * KB, BF16, 8, 1024)]
    W_PROJ = view(148 * KB, BF16, 2, 1024)
    GB1 = view(152 * KB, F32, 1024)
    A3 = [view(172 * KB + s * 4 * KB, F32, 1024) for s in range(2)]
    JUNK3 = view(180 * KB, BF16, 1024)
    RL = [view(204 * KB, F32, 512), view(152 * KB, F32, 512)]
    GB2 = view(116 * KB, F32, 1024)
    A5 = [view(120 * KB + s * 4 * KB, F32, 1024) for s in range(2)]
    GT = [view(128 * KB + s * 2 * KB, BF16, 8, 128) for s in range(2)]
    PS_ = [view(132 * KB + s * KB, F32, 256) for s in range(2)]
    PT = [view(134 * KB + s * 512, BF16, 2, 128) for s in range(2)]
    SG = [view(136 * KB + s * 2 * KB, F32, 512) for s in range(2)]
    OUTB = [view(156 * KB + s * 4 * KB, F32, 1024) for s in range(2)]
    JUNK5 = view(164 * KB, BF16, 1024)

    free = {}

    def fr(k):
        return free.get(k)

    d_c = DSem(nc, "c", stack)
    tk_c1 = d_c.dma(sp, identf, cst[:, 0:128])
    tk_c2 = d_c.dma(sp, gpt, gp[:, :])
    tk_c3 = d_c.dma(sp, GB0, gbc[0])
    d_cb = DSem(nc, "cb", stack)
    tk_cb = d_cb.dma(pool, cstb, cst[:, :])
    tk_zl = pool.tick(pool.e.memset(zl, 0.0))
    d_wkv = DSem(nc, "wkv", stack)
    d_win = DSem(nc, "win", stack)
    for kc in range(8):
        tk_wkv = d_wkv.dma(pool, W_IN[:, kc, 2048:3072], w_in[kc * 128:(kc + 1) * 128, 2048:3072])
    for c0 in (0, 1024):
        for kc in range(8):
            tk_win = d_win.dma(pool, W_IN[:, kc, c0:c0 + 1024], w_in[kc * 128:(kc + 1) * 128, c0:c0 + 1024])
    tk_consts = [tk_c3, tk_cb]

    stat_n = [0]

    def stat3():
        i = stat_n[0] % 30
        stat_n[0] += 1
        return stat[:, 3 * i:3 * i + 1], stat[:, 3 * i + 1:3 * i + 2], stat[:, 3 * i + 2:3 * i + 3]

    d_xs = [DSem(nc, "xs%d" % s, stack) for s in range(4)]
    xs_n = [0]
    tp_n = [0]

    def frontA(rows_ap, npart, gb, junk):
        s = xs_n[0] % 4
        xs_n[0] += 1
        xt = XS[s][0:npart, :]
        sp.wait(fr(("xs", s)))
        ld = d_xs[s].dma(sp, xt, rows_ap)
        ssq, tt, rs = stat3()
        ssq, tt, rs = ssq[0:npart, :], tt[0:npart, :], rs[0:npart, :]
        act.wait(ld, tk_consts)
        a1 = act.tick(act.e.activation(out=junk[0:npart, :], in_=xt, func=AF.Square, accum_out=ssq))
        act.wait(a1)
        a2 = act.tick(act.e.activation(out=tt, in_=ssq, func=AF.Ln, scale=1.0 / 1024.0, bias=EPS))
        act.wait(a2)
        a3 = act.tick(act.e.activation(out=rs, in_=tt, func=AF.Exp, scale=-0.5))
        vec.wait(a3, ld, tk_consts)
        v1 = vec.tick(vec.e.scalar_tensor_tensor(out=xt, in0=xt, scalar=rs, in1=gb[0:npart, :],
                                                 op0=ALU.mult, op1=ALU.mult))
        return dict(s=s, xt=xt, v1=v1, npart=npart)

    def frontB(c, dst, extra_dst_wait=None):
        s, xt, v1, npart = c["s"], c["xt"], c["v1"], c["npart"]
        pr = tp_n[0] % 2
        tp_n[0] += 1
        tpb = ps[:, pr * 1024:(pr + 1) * 1024]
        pe.wait(v1, fr(("tp", pr)), tk_consts)
        for kc in range(8):
            ins = pe.e.transpose(out=tpb[:, kc * npart:(kc + 1) * npart], in_=xt[:, kc * 128:(kc + 1) * 128],
                                 identity=identf[0:npart, 0:npart])
        p1 = pe.tick(ins)
        free[("xs", s)] = p1
        act.wait(p1, extra_dst_wait)
        if npart == 128:
            act.tick(act.e.activation(out=dst[:, 0:4, :], in_=tpb[:, 0:512].rearrange("p (a b) -> p a b", a=4),
                                      func=AF.Copy))
            e1 = act.tick(act.e.activation(out=dst[:, 4:8, :],
                                           in_=tpb[:, 512:1024].rearrange("p (a b) -> p a b", a=4), func=AF.Copy))
        else:
            e1 = act.tick(act.e.activation(out=dst, in_=tpb[:, 0:8 * npart].rearrange("p (a b) -> p a b", a=8),
                                           func=AF.Copy))
        free[("tp", pr)] = e1
        return e1

    def front(rows_ap, npart, gb, dst, junk, extra_dst_wait=None):
        return frontB(frontA(rows_ap, npart, gb, junk), dst, extra_dst_wait)

    ring_n = [0]

    def ring_bank():
        b = 4 + ring_n[0] % 4
        ring_n[0] += 1
        return b

    def mm_group(b, pairs, cols=None):
        out = bank(b) if cols is None else bank(b)[:, cols[0]:cols[1]]
        pe.wait(fr(("bank", b)))
        n = len(pairs)
        for i, (l, r_) in enumerate(pairs):
            ins = pe.e.matmul(out, l, r_, start=(i == 0), stop=(i == n - 1))
        return pe.tick(ins)

    def kv_after(j, e1):
        buf = j % 2
        pe.wait(e1, tk_wkv)
        for m in range(4):
            b = ring_bank()
            t = mm_group(b, [(W_IN[:, kc, 2048 + m * 128:2048 + (m + 1) * 128], AT[buf][:, kc, :]) for kc in range(8)])
            vec.wait(t)
            free[("bank", b)] = vec.tick(vec.e.tensor_copy(out=KT[:, m, j * 512:(j + 1) * 512], in_=bank(b)))
        for cc in range(4):
            b = ring_bank()
            t = mm_group(b, [(AT[buf][:, kc, cc * 128:(cc + 1) * 128], W_IN[:, kc, 2560:3072]) for kc in range(8)])
            vec.wait(t)
            free[("bank", b)] = vec.tick(vec.e.tensor_copy(out=V[:, j * 4 + cc, :], in_=bank(b)))
        free[("at", buf)] = t

    halo_tk = {}

    def halo_block():
        e1 = front(xh[:, :], 8, GB0, aTh, JUNK)
        pe.wait(e1, tk_win)
        hb = ring_bank()
        pe.wait(fr(("bank", hb)))
        for m8 in range(8):
            col0 = 512 + m8 * 128
            for kc in range(8):
                ins = pe.e.matmul(bank(hb)[:, m8 * 8:(m8 + 1) * 8], W_IN[:, kc, col0:col0 + 128], aTh[:, kc, :],
                                  start=(kc == 0), stop=(kc == 7))
        t = pe.tick(ins)
        act.wait(t)
        a = act.tick(act.e.activation(out=cch, in_=bank(hb)[:, 0:32].rearrange("p (a b) -> p a b", a=4), func=AF.Copy))
        vec.wait(a)
        free[("bank", hb)] = vec.tick(vec.e.tensor_tensor(out=vhalo, in0=cch,
                                                          in1=bank(hb)[:, 32:64].rearrange("p (a b) -> p a b", a=4),
                                                          op=ALU.mult))
        halo_tk["v"] = free[("bank", hb)]

    def own_after(i, e1):
        buf = i % 2
        if i == 0:
            halo_block()
        pe.wait(e1)
        tsl = slice(i * 512, (i + 1) * 512)
        for c in range(4):
            tks = []
            bs = []
            for col0 in (c * 128, 512 + c * 128, 1024 + c * 128):
                b = ring_bank()
                bs.append(b)
                tks.append(mm_group(b, [(W_IN[:, kc, col0:col0 + 128], AT[buf][:, kc, :]) for kc in range(8)]))
            act.wait(tks[0], fr("cb_sb"))
            free[("bank", bs[0])] = a_cb = act.tick(act.e.activation(out=cb_sb, in_=bank(bs[0]), func=AF.Copy))
            act.wait(tks[1], fr("cc_sb"))
            free[("bank", bs[1])] = a_cc = act.tick(act.e.activation(out=cc_sb, in_=bank(bs[1]), func=AF.Copy))
            act.wait(tks[2], fr("cu_sb"))
            free[("bank", bs[2])] = a_cu = act.tick(act.e.activation(out=cu_sb, in_=bank(bs[2]), func=AF.Copy))
            pool.wait(a_cc, a_cu, fr("vbuf"), halo_tk["v"])
            pool.tick(pool.e.tensor_copy(out=vbuf[:, 0:2], in_=vhalo[:, c, 2 * i:2 * i + 2]))
            g1 = pool.tick(pool.e.tensor_tensor(out=vbuf[:, 2:514], in0=cc_sb, in1=cu_sb, op=ALU.mult))
            free["cc_sb"] = g1
            free["cu_sb"] = g1
            w0 = gpt[:, 8 + c * 3:9 + c * 3]
            w1 = gpt[:, 9 + c * 3:10 + c * 3]
            w2 = gpt[:, 10 + c * 3:11 + c * 3]
            vec.wait(g1, fr("ybuf"), tk_consts)
            d1 = vec.tick(vec.e.tensor_scalar(out=ybuf, in0=vbuf[:, 0:512], scalar1=w0, scalar2=None, op0=ALU.mult))
            vec.wait(d1)
            d2 = vec.tick(vec.e.scalar_tensor_tensor(out=ybuf, in0=vbuf[:, 1:513], scalar=w1, in1=ybuf,
                                                     op0=ALU.mult, op1=ALU.add))
            vec.wait(d2)
            d3 = vec.tick(vec.e.scalar_tensor_tensor(out=ybuf, in0=vbuf[:, 2:514], scalar=w2, in1=ybuf,
                                                     op0=ALU.mult, op1=ALU.add))
            free["vbuf"] = d3
            vec.wait(d3, a_cb)
            d4 = vec.tick(vec.e.tensor_tensor(out=cb_sb, in0=ybuf, in1=cb_sb, op=ALU.mult))
            free["ybuf"] = d4
            act.wait(d4, fr("ysq"))
            a_sq = act.tick(act.e.activation(out=ysq, in_=cb_sb, func=AF.Square))
            b = ring_bank()
            pe.wait(a_sq, tk_consts)
            t_ms = mm_group(b, [(bd, ysq)])
            free["ysq"] = t_ms
            act.wait(t_ms, fr("tbuf"))
            a_ln = act.tick(act.e.activation(out=tbuf, in_=bank(b), func=AF.Ln, bias=EPS))
            free[("bank", b)] = a_ln
            act.wait(a_ln, fr("rbuf"))
            a_rs = act.tick(act.e.activation(out=rbuf, in_=tbuf, func=AF.Exp, scale=-0.5))
            free["tbuf"] = a_rs
            vec.wait(a_rs)
            d5 = vec.tick(vec.e.scalar_tensor_tensor(out=MIX[:, c, tsl], in0=cb_sb, scalar=gpt[:, c:c + 1], in1=rbuf,
                                                     op0=ALU.mult, op1=ALU.mult))
            free["cb_sb"] = d5
            free["rbuf"] = d5
        for m in range(4):
            b = ring_bank()
            t = mm_group(b, [(W_IN[:, kc, 1536 + m * 128:1536 + (m + 1) * 128], AT[buf][:, kc, :]) for kc in range(8)])
            vec.wait(t)
            free[("bank", b)] = vec.tick(vec.e.tensor_scalar(out=QT[:, m, tsl], in0=bank(b), scalar1=0.125,
                                                             scalar2=None, op0=ALU.mult))
        free[("at", buf)] = t

    chunks = []
    for j in range(8):
        for cc in range(4):
            r0 = (j * 4 + cc) * 128
            chunks.append(dict(rows=xa[r0:r0 + 128, :], buf=j % 2, cc=cc,
                               after=(lambda e1, j=j: kv_after(j, e1)) if cc == 3 else None))
    for i in range(4):
        for cc in range(4):
            r0 = (i * 4 + cc) * 128
            chunks.append(dict(rows=xo[r0:r0 + 128, :], buf=i % 2, cc=cc,
                               after=(lambda e1, i=i: own_after(i, e1)) if cc == 3 else None))
    LOOK = 2
    ctxs = {}
    for n in range(min(LOOK, len(chunks))):
        ctxs[n] = frontA(chunks[n]["rows"], 128, GB0, JUNK)
    for n, ch in enumerate(chunks):
        if n + LOOK < len(chunks):
            ctxs[n + LOOK] = frontA(chunks[n + LOOK]["rows"], 128, GB0, JUNK)
        e1 = frontB(ctxs.pop(n), AT[ch["buf"]][:, :, ch["cc"] * 128:(ch["cc"] + 1) * 128], fr(("at", ch["buf"])))
        if ch["after"] is not None:
            ch["after"](e1)


    p1_done = [pe.last, act.last, vec.last, pool.last]

    d_neg = DSem(nc, "neg", stack)
    d_wo = DSem(nc, "wo", stack)
    d_wg = DSem(nc, "wg", stack)
    pool.wait(p1_done)
    for g in range(4):
        tk_neg = d_neg.dma(pool, NEG[:, g * 4:(g + 1) * 4, :], neg[:, g * 4:(g + 1) * 4, :])
    for kc in range(8):
        tk_wo = d_wo.dma(pool, W_OUT[:, kc, :], w_out[kc * 128:(kc + 1) * 128, :])
    for kc in range(8):
        tk_wg = d_wg.dma(pool, W_GATE[:, kc, :], w_gate[kc * 128:(kc + 1) * 128, :])

    units = []
    for i in range(4):
        ntile = 2 * i + 2
        for h in range(8):
            ul = []
            for tix in range(ntile - 1, -1, -1):
                for b in range(3, -1, -1):
                    top = (tix == ntile - 1)
                    second = (tix == ntile - 2)
                    c0 = 128 * b if top else 0
                    mk = None
                    if top:
                        mk = NEG[:, (i % 2) * 8 + b, :]
                    elif second:
                        mk = NEG[:, (i % 2) * 8 + 4 + b, :]
                    ul.append(dict(i=i, h=h, kb=tix * 4 + b, c0=c0, mk=mk))
            for u, d in enumerate(ul):
                d["u"] = u
                d["first"] = (u == 0)
                d["last"] = (u == len(ul) - 1)
            units.extend(ul)
    for k, d in enumerate(units):
        d["k"] = k

    SB = [0, 1, 2]
    SB2 = [6, 7]
    OB = [3, 4]
    MSB = 5
    deferred = []

    def st_z(d):
        k, i, h, c0 = d["k"], d["i"], d["h"], d["c0"]
        hc, hp = h // 2, (h % 2) * 64
        sb = SB[k % 3]
        out = bank(sb)[:, c0:512]
        pe.wait(fr(("bank", sb)), p1_done)
        kk = d["kb"] * 128
        has_mk = d["mk"] is not None
        ins = pe.e.matmul(out, KT[hp:hp + 64, hc, kk:kk + 128], QT[hp:hp + 64, hc, i * 512 + c0:(i + 1) * 512],
                          start=True, stop=not has_mk)
        if has_mk:
            pe.wait(tk_neg)
            ins = pe.e.matmul(out, identb, d["mk"][:, c0:512], start=False, stop=True)
        d["t_z"] = pe.tick(ins)

    def st_e1(d):
        k, c0 = d["k"], d["c0"]
        sb = SB[k % 3]
        act.wait(d["t_z"], fr(("et", k % 4)))
        d["t_e1"] = act.tick(act.e.activation(out=ET[k % 4][:, c0:512], in_=bank(sb)[:, c0:512], func=AF.Exp))
        free[("bank", sb)] = d["t_e1"]

    def st_ln(d):
        k, c0 = d["k"], d["c0"]
        act.wait(d["t_e1"], fr(("lt", k % 4)))
        d["t_ln"] = act.tick(act.e.activation(out=LT[k % 4][:, c0:512], in_=ET[k % 4][:, c0:512], func=AF.Ln, bias=1.0))

    def st_tc(d):
        k, c0, h = d["k"], d["c0"], d["h"]
        sb = SB2[k % 2]
        out = bank(sb)[:, c0:512]
        pe.wait(d["t_ln"], tk_consts, fr(("bank", sb)))
        ins = pe.e.matmul(out, ntri, LT[k % 4][:, c0:512], start=True, stop=d["first"])
        if not d["first"]:
            pe.wait(fr(("lsum_w", h % 2)))
            ins = pe.e.matmul(out, nones, LSUM[h % 2][:, c0:512], start=False, stop=True)
        d["t_tc"] = pe.tick(ins)

    def st_add(d):
        k, c0, h = d["k"], d["c0"], d["h"]
        if d["first"]:
            pool.wait(fr(("lsum_r", h % 2)))
            free[("lsum_w", h % 2)] = pool.tick(pool.e.memset(LSUM[h % 2], 0.0))
        if d["last"]:
            free[("lt", k % 4)] = d["t_tc"]
            free[("lsum_r", h % 2)] = d["t_tc"]
            return
        pool.wait(d["t_ln"], d["t_tc"], fr(("lsum_w", h % 2)))
        t = pool.tick(pool.e.tensor_tensor(out=LSUM[h % 2][:, c0:512], in0=LSUM[h % 2][:, c0:512],
                                           in1=LT[k % 4][:, c0:512], op=ALU.add))
        free[("lsum_w", h % 2)] = t
        free[("lt", k % 4)] = [t, d["t_tc"]]

    def st_e3(d):
        k, c0 = d["k"], d["c0"]
        sb = SB2[k % 2]
        act.wait(d["t_tc"], fr(("px", k % 2)))
        d["t_e3"] = act.tick(act.e.activation(out=PX[k % 2][:, c0:512], in_=bank(sb)[:, c0:512], func=AF.Exp))
        free[("bank", sb)] = d["t_e3"]
        vec.wait(d["t_e3"], d["t_e1"], fr(("att", k % 4)))
        d["t_m"] = vec.tick(vec.e.tensor_tensor(out=ATT[k % 4][:, c0:512], in0=ET[k % 4][:, c0:512],
                                                in1=PX[k % 2][:, c0:512], op=ALU.mult))
        free[("px", k % 2)] = d["t_m"]
        free[("et", k % 4)] = [d["t_m"], d["t_ln"]]

    def st_av(d):
        k, c0, h, i = d["k"], d["c0"], d["h"], d["i"]
        pair = (i * 8 + h) // 2
        ob = OB[pair % 2]
        hp = (h % 2) * 64
        if d["first"]:
            if h % 2 == 0:
                pe.wait(fr(("bank", ob)))
            pe.wait(tk_zl, tk_consts)
            pe.e.matmul(bank(ob)[hp:hp + 64, :], zl, cstb, start=True, stop=False)
        pe.wait(d["t_m"])
        ins = pe.e.matmul(bank(ob)[hp:hp + 64, c0:512], V[:, d["kb"], h * 64:(h + 1) * 64], ATT[k % 4][:, c0:512],
                          start=False, stop=d["last"])
        t = pe.tick(ins)
        free[("att", k % 4)] = t
        if d["last"] and h % 2 == 1:
            fin_pair(i, h // 2, pair, ob, t)

    def fin_pair(i, hpair, pair, ob, t_av):
        s2 = pair % 2
        tsl = slice(i * 512, (i + 1) * 512)

        def f1():
            vec.wait(t_av, fr(("osb", s2)))
            c1 = vec.tick(vec.e.tensor_copy(out=OSB[s2], in_=bank(ob)))
            free[("bank", ob)] = c1
            pool.wait(c1, fr(("osq", s2)))
            q1 = pool.tick(pool.e.tensor_tensor(out=OSQ[s2], in0=OSB[s2], in1=OSB[s2], op=ALU.mult))

            def f2():
                pe.wait(q1, fr(("bank", MSB)))
                ins = pe.e.matmul(bank(MSB), bd, OSQ[s2], start=True, stop=True)
                t_ms = pe.tick(ins)
                free[("osq", s2)] = t_ms

                def f3():
                    act.wait(t_ms, fr(("tb2", s2)))
                    a_ln = act.tick(act.e.activation(out=TB2[s2], in_=bank(MSB), func=AF.Ln, bias=EPS))
                    free[("bank", MSB)] = a_ln
                    act.wait(a_ln)
                    a_rs = act.tick(act.e.activation(out=TB2[s2], in_=TB2[s2], func=AF.Exp, scale=-0.5))
                    vec.wait(a_rs)
                    d5 = vec.tick(vec.e.scalar_tensor_tensor(out=MIX[:, 4 + hpair, tsl], in0=OSB[s2],
                                                             scalar=gpt[:, 4 + hpair:5 + hpair], in1=TB2[s2],
                                                             op0=ALU.mult, op1=ALU.mult))
                    free[("osb", s2)] = d5
                    free[("tb2", s2)] = d5
                deferred.append([1, f3])
            deferred.append([1, f2])
        deferred.append([0, f1])

    def run_deferred(flush=False):
        while True:
            ready = [x for x in deferred if x[0] <= 0]
            if not ready:
                if flush and deferred:
                    for x in deferred:
                        x[0] -= 1
                    continue
                break
            for x in ready:
                deferred.remove(x)
                x[1]()
        for x in deferred:
            x[0] -= 1

    NU = len(units)
    st_z(units[0])
    for k in range(NU + 1):
        if k + 1 < NU:
            st_z(units[k + 1])
        if k < NU:
            st_e1(units[k])
        if k >= 1:
            st_e3(units[k - 1])
        if k < NU:
            st_ln(units[k])
            st_tc(units[k])
            st_add(units[k])
        if k >= 1:
            st_av(units[k - 1])
        run_deferred()
    run_deferred(flush=True)

    p2_done = [pe.last, act.last, vec.last, pool.last]

    d_h = [DSem(nc, "h%d" % s, stack) for s in range(16)]
    d_wp = DSem(nc, "wp", stack)
    d_g1 = DSem(nc, "g1", stack)
    d_ml = [DSem(nc, "ml%d" % s, stack) for s in range(2)]

    def load_quarter(qq, bufq, waits):
        pool.wait(waits)
        for kc in range(8):
            d_ml[bufq].dma(pool, UP[bufq][:, kc, :], w_up[kc * 128:(kc + 1) * 128, qq * 1024:(qq + 1) * 1024])
        for fc in range(8):
            tk = d_ml[bufq].dma(pool, DN[bufq][:, fc, :], w_dn[qq * 1024 + fc * 128:qq * 1024 + (fc + 1) * 128, :])
        return tk

    pool.wait(p2_done)
    for kc in range(2):
        tk_wp = d_wp.dma(pool, W_PROJ[:, kc, :], w_proj[kc * 128:(kc + 1) * 128, :])
    tk_q = [None] * 4
    tk_q[0] = load_quarter(0, 0, None)
    sp.wait(p2_done)
    tk_g1 = d_g1.dma(sp, GB1, gbc[1])
    tk_h = []
    for oc in range(16):
        tk_h.append(d_h[oc].dma(sp, H[oc], xo[oc * 128:(oc + 1) * 128, :]))
    t_h1 = [None] * 16
    for oc in range(16):
        for hf in range(2):
            b = ring_bank()
            pe.wait(tk_wo, p2_done)
            t = mm_group(b, [(MIX[:, kc, oc * 128:(oc + 1) * 128], W_OUT[:, kc, hf * 512:(hf + 1) * 512])
                             for kc in range(8)])
            vec.wait(t, tk_h[oc])
            hs = H[oc][:, hf * 512:(hf + 1) * 512]
            t_h1[oc] = free[("bank", b)] = vec.tick(vec.e.tensor_tensor(out=hs, in0=hs, in1=bank(b), op=ALU.add))
    p3a_pe = pe.last

    a3_n = [0]

    def front3A(src, gb, junk, abufs, src_tk, gb_tk):
        s = a3_n[0] % 2
        a3_n[0] += 1
        xt = abufs[s]
        ssq, tt, rs = stat3()
        act.wait(src_tk)
        a1 = act.tick(act.e.activation(out=junk, in_=src, func=AF.Square, accum_out=ssq))
        act.wait(a1)
        a2 = act.tick(act.e.activation(out=tt, in_=ssq, func=AF.Ln, scale=1.0 / 1024.0, bias=EPS))
        act.wait(a2)
        a3 = act.tick(act.e.activation(out=rs, in_=tt, func=AF.Exp, scale=-0.5))
        vec.wait(a3, src_tk, fr(("a3", id(abufs), s)), gb_tk)
        v1 = vec.tick(vec.e.scalar_tensor_tensor(out=xt, in0=src, scalar=rs, in1=gb, op0=ALU.mult, op1=ALU.mult))
        return dict(s=s, xt=xt, v1=v1, key=("a3", id(abufs), s))

    def front3B(c, dst, dst_wait):
        xt, v1 = c["xt"], c["v1"]
        pr = tp_n[0] % 2
        tp_n[0] += 1
        tpb = ps[:, pr * 1024:(pr + 1) * 1024]
        pe.wait(v1, fr(("tp", pr)))
        for kc in range(8):
            ins = pe.e.transpose(out=tpb[:, kc * 128:(kc + 1) * 128], in_=xt[:, kc * 128:(kc + 1) * 128],
                                 identity=identf)
        p1 = pe.tick(ins)
        free[c["key"]] = p1
        act.wait(p1, dst_wait)
        act.tick(act.e.activation(out=dst[:, 0:4, :], in_=tpb[:, 0:512].rearrange("p (a b) -> p a b", a=4),
                                  func=AF.Copy))
        e1 = act.tick(act.e.activation(out=dst[:, 4:8, :], in_=tpb[:, 512:1024].rearrange("p (a b) -> p a b", a=4),
                                       func=AF.Copy))
        free[("tp", pr)] = e1
        return e1

    c3 = front3A(H[0], GB1, JUNK3, A3, t_h1[0], tk_g1)
    for oc in range(16):
        nx = front3A(H[oc + 1], GB1, JUNK3, A3, t_h1[oc + 1], tk_g1) if oc + 1 < 16 else None
        t_mt = front3B(c3, MT[:, :, oc * 128:(oc + 1) * 128], p3a_pe)
        c3 = nx
    mt_done = [pe.last, act.last, vec.last]

    ut_n = [0]
    rl_n = [0]
    for qq in range(4):
        bq = qq % 2
        if qq + 1 < 4:
            tk_q[qq + 1] = load_quarter(qq + 1, 1 - bq, mt_done if qq == 0 else fr(("mlq", 1 - bq)))
        for i in range(4):
            ub = ut_n[0] % 2
            ut_n[0] += 1
            pe.wait(tk_q[qq], t_mt)
            for fc in range(8):
                b = ring_bank()
                t = mm_group(b, [(UP[bq][:, kc, fc * 128:(fc + 1) * 128], MT[:, kc, i * 512:(i + 1) * 512])
                                 for kc in range(8)])
                rs_ = rl_n[0] % 2
                rl_n[0] += 1
                act.wait(t, fr(("rl", rs_)))
                a_r = act.tick(act.e.activation(out=RL[rs_], in_=bank(b), func=AF.Relu))
                free[("bank", b)] = a_r
                pool.wait(a_r, fr(("ut", ub)) if fc == 0 else None)
                g_ = pool.tick(pool.e.tensor_tensor(out=UT[ub][:, fc, :], in0=RL[rs_], in1=RL[rs_], op=ALU.mult))
                free[("rl", rs_)] = g_
            pe.wait(g_)
            for cc in range(4):
                oc = i * 4 + cc
                for hf in range(2):
                    b = ring_bank()
                    t = mm_group(b, [(UT[ub][:, fc, cc * 128:(cc + 1) * 128], DN[bq][:, fc, hf * 512:(hf + 1) * 512])
                                     for fc in range(8)])
                    vec.wait(t, t_h1[oc])
                    hs = H[oc][:, hf * 512:(hf + 1) * 512]
                    t_h1[oc] = free[("bank", b)] = vec.tick(vec.e.tensor_tensor(out=hs, in0=hs, in1=bank(b), op=ALU.add))
            free[("ut", ub)] = t
        free[("mlq", bq)] = pe.last
    p3b_done = [pe.last, act.last, vec.last, pool.last]

    d_g2 = DSem(nc, "g2", stack)
    d_g3 = DSem(nc, "g3", stack)
    d_p = [DSem(nc, "p%d" % s, stack) for s in range(2)]
    d_o = [DSem(nc, "o%d" % s, stack) for s in range(2)]
    SG4 = [view(136 * KB + s * 2 * KB, F32, 512) for s in range(4)]
    sp.wait(p3b_done)
    act.wait(p3b_done)
    vec.wait(p3b_done)
    pe.wait(p3b_done)
    tk_g2 = d_g2.dma(sp, GB2, gbc[2])
    tk_g3 = d_g3.dma(sp, GB1, gbc[3])
    o_tk = [None, None]

    def pleA(oc):
        s2 = oc % 2
        c = front3A(H[oc], GB2, JUNK5, A5, t_h1[oc], tk_g2)
        sp.wait(fr(("ps", s2)))
        c["tk_p"] = d_p[s2].dma(sp, PS_[s2], po[oc * 128:(oc + 1) * 128, :])
        return c

    def pleB(oc, c):
        s2 = oc % 2
        e1 = front3B(c, GT[s2], fr(("gt", s2)))
        b = ring_bank()
        pe.wait(c["tk_p"], fr(("bank", b)))
        for k2 in range(2):
            ins = pe.e.transpose(out=bank(b)[:, k2 * 128:(k2 + 1) * 128], in_=PS_[s2][:, k2 * 128:(k2 + 1) * 128],
                                 identity=identf)
        t = pe.tick(ins)
        free[("ps", s2)] = t
        vec.wait(t, fr(("pt", s2)))
        free[("bank", b)] = c_pt = vec.tick(vec.e.tensor_copy(out=PT[s2],
                                                             in_=bank(b)[:, 0:256].rearrange("p (a b) -> p a b", a=2)))
        pe.wait(e1, c_pt, tk_wg, tk_wp)
        for hf in range(2):
            bg = ring_bank()
            t_g = mm_group(bg, [(GT[s2][:, kc, :], W_GATE[:, kc, hf * 512:(hf + 1) * 512]) for kc in range(8)])
            bp = ring_bank()
            t_p = mm_group(bp, [(PT[s2][:, k2, :], W_PROJ[:, k2, hf * 512:(hf + 1) * 512]) for k2 in range(2)])
            sg = SG4[s2 * 2 + hf]
            act.wait(t_g, fr(("sg", s2 * 2 + hf)))
            a_e = act.tick(act.e.activation(out=sg, in_=bank(bg), func=AF.Exp, scale=-1.0))
            free[("bank", bg)] = a_e
            act.wait(a_e)
            a_l = act.tick(act.e.activation(out=sg, in_=sg, func=AF.Ln, bias=1.0))
            act.wait(a_l)
            a_s = act.tick(act.e.activation(out=sg, in_=sg, func=AF.Exp, scale=-1.0))
            vec.wait(a_s, t_p)
            v3 = vec.tick(vec.e.tensor_tensor(out=sg, in0=sg, in1=bank(bp), op=ALU.mult))
            free[("bank", bp)] = v3
            hs = H[oc][:, hf * 512:(hf + 1) * 512]
            vec.wait(v3, t_h1[oc])
            v4 = vec.tick(vec.e.tensor_tensor(out=hs, in0=hs, in1=sg, op=ALU.add))
            free[("sg", s2 * 2 + hf)] = v4
        free[("gt", s2)] = t_g
        free[("pt", s2)] = t_p
        ssq, tt, rs = stat3()
        act.wait(v4)
        a1 = act.tick(act.e.activation(out=JUNK5, in_=H[oc], func=AF.Square, accum_out=ssq))
        act.wait(a1)
        a2 = act.tick(act.e.activation(out=tt, in_=ssq, func=AF.Ln, scale=1.0 / 1024.0, bias=EPS))
        act.wait(a2)
        a3 = act.tick(act.e.activation(out=rs, in_=tt, func=AF.Exp, scale=-0.5))
        vec.wait(a3, o_tk[s2], tk_g3)
        v5 = vec.tick(vec.e.scalar_tensor_tensor(out=OUTB[s2], in0=H[oc], scalar=rs, in1=GB1,
                                                 op0=ALU.mult, op1=ALU.mult))
        sp.wait(v5)
        o_tk[s2] = d_o[s2].dma(sp, y[oc * 128:(oc + 1) * 128, :], OUTB[s2])

    cP = pleA(0)
    for oc in range(16):
        nx = pleA(oc + 1) if oc + 1 < 16 else None
        pleB(oc, cP)
        cP = nx
    sp.wait(o_tk[0], o_tk[1])
    stack.close()
    return nc


def _consts():
    c = np.zeros((128, 512), np.float32)
    j = np.arange(128)[:, None]
    s = np.arange(128)[None, :]
    c[:, 0:128] = (j == s)
    c[:, 128:256] = np.where(j >= s, -1.0, 0.0)
    c[:, 256:384] = -1.0
    c[:, 384:512] = np.where((j // 64) == (s // 64), 1.0 / 64.0, 0.0)
    return c


def _neg(r):
    out = np.zeros((128, 16, 512), np.float32)
    s = np.arange(128)[:, None]
    t = np.arange(512)[None, :]
    for e in range(2):
        typ = "A" if (e + r) % 2 == 1 else "B"
        for b in range(4):
            tri = np.where((128 * b + s) < t, 0.0, NEGV).astype(np.float32)
            if typ == "A":
                out[:, e * 8 + b, :] = tri
                out[:, e * 8 + 4 + b, :] = 0.0
            else:
                out[:, e * 8 + b, :] = NEGV
                out[:, e * 8 + 4 + b, :] = tri
    return out


_NC_CACHE = {}


def kernel(x, p, g_mix, w_in, conv_w, g_conv_out, g_attn_out, w_out, g_mlp, w_up, w_down,
           g_ple, w_ple_gate, w_ple_proj, g_final):
    f = lambda a: np.ascontiguousarray(np.asarray(a, dtype=np.float32))
    x, p = f(x), f(p)
    g_mix, g_mlp, g_ple, g_final = f(g_mix), f(g_mlp), f(g_ple), f(g_final)
    conv_w, g_conv_out, g_attn_out = f(conv_w), f(g_conv_out), f(g_attn_out)
    wi, wo, wu, wd, wg, wp = f(w_in)[0], f(w_out)[0], f(w_up)[0], f(w_down)[0], f(w_ple_gate)[0], f(w_ple_proj)[0]

    gp = np.zeros((128, 20), np.float32)
    gp[:, 0:4] = g_conv_out[0].reshape(4, 128).T
    gp[:, 4:8] = g_attn_out[0].reshape(4, 128).T
    for c in range(4):
        for k in range(3):
            gp[:, 8 + c * 3 + k] = conv_w[0, k, c * 128:(c + 1) * 128]
    gbc = np.stack([np.broadcast_to(v, (128, 1024)) for v in (g_mix[0], g_mlp[0], g_ple[0], g_final)]).astype(np.float32)
    gbc = np.ascontiguousarray(gbc)
    cst = _consts()

    in_maps = []
    for c in range(8):
        b, r = c // 2, c % 2
        tiles = OWN[r]
        xo = np.concatenate([x[b, t * 512:(t + 1) * 512] for t in tiles], axis=0)
        xh = np.zeros((8, 1024), np.float32)
        for i, t in enumerate(tiles):
            if t > 0:
                xh[2 * i:2 * i + 2] = x[b, t * 512 - 2:t * 512]
        po = np.concatenate([p[0, b, t * 512:(t + 1) * 512] for t in tiles], axis=0)
        in_maps.append({
            "xa": x[b], "xo": np.ascontiguousarray(xo), "xh": xh, "po": np.ascontiguousarray(po),
            "w_in": wi, "w_out": wo, "w_up": wu, "w_dn": wd, "w_gate": wg, "w_proj": wp,
            "cst": cst, "gp": gp, "gbc": gbc, "neg": _neg(r),
        })
    if "nc" not in _NC_CACHE:
        _NC_CACHE["nc"] = build_nc()
    nc = _NC_CACHE["nc"]
    res = run_bass_kernel_spmd(nc, in_maps, core_ids=list(range(8)))
    out = np.zeros((4, 4096, 1024), np.float32)
    for c in range(8):
        b, r = c // 2, c % 2
        yc = np.asarray(res.results[c]["y"], dtype=np.float32)
        for i, t in enumerate(OWN[r]):
            out[b, t * 512:(t + 1) * 512] = yc[i * 512:(i + 1) * 512]
    return out
```

```python
import numpy as np
import ml_dtypes
from contextlib import ExitStack
import concourse.bass as bass
import concourse.mybir as mybir
from concourse.bass_utils import run_bass_kernel_spmd

F32 = mybir.dt.float32
BF16 = mybir.dt.bfloat16
AF = mybir.ActivationFunctionType
ALU = mybir.AluOpType

EPS = 1e-6
NEGV = -30000.0
KB = 1024
OWN = {0: [0, 3, 4, 7], 1: [1, 2, 5, 6]}
DEBUG = False


_USED = None
_WAITED = set()


class Eng:
    def __init__(self, nc, e, name, stack):
        self.e = e
        self.name = name
        self.sem = stack.enter_context(nc.semaphore("s_" + name))
        self.ord = 0
        self.cnt = 0
        self.val = {}
        self.seen = {}
        self.last = None

    def wait(self, *tks):
        for tk in tks:
            if tk is None:
                continue
            if isinstance(tk, list):
                self.wait(*tk)
                continue
            if tk[0] == "E":
                _, eng, o = tk
                if self.seen.get(eng.name, 0) >= o:
                    continue
                _WAITED.add((eng.name, o))
                self.e.wait_ge(eng.sem, eng.val[o])
                self.seen[eng.name] = o
            else:
                _, sem, v, key = tk
                if self.seen.get(key, 0) >= v:
                    continue
                self.e.wait_ge(sem, v)
                self.seen[key] = v

    def tick(self, ins):
        self.ord += 1
        if _USED is None or (self.name, self.ord) in _USED:
            ins.then_inc(self.sem, 1)
            self.cnt += 1
            self.val[self.ord] = self.cnt
        self.last = ("E", self, self.ord)
        return self.last


class DSem:
    def __init__(self, nc, name, stack):
        self.sem = stack.enter_context(nc.semaphore("d_" + name))
        self.n = 0
        self.key = "d_" + name

    def dma(self, q, out, in_):
        ins = q.e.dma_start(out=out, in_=in_)
        ins.then_inc(self.sem, 16)
        self.n += 16
        return ("D", self.sem, self.n, self.key)


def build_nc():
    global _USED, _WAITED
    _USED = None
    _WAITED = set()
    _build()
    _USED = set(_WAITED)
    _WAITED = set()
    nc = _build()
    assert _WAITED == _USED, (len(_WAITED), len(_USED))
    return nc


def _build():
    nc = bass.Bass("TRN2", target_bir_lowering=False)
    stack = ExitStack()

    def D(name, shape, kind="ExternalInput"):
        return nc.dram_tensor(name, shape, F32, kind=kind).ap()

    xa = D("xa", [4096, 1024])
    xo = D("xo", [2048, 1024])
    xh = D("xh", [8, 1024])
    po = D("po", [2048, 256])
    w_in = D("w_in", [1024, 3072])
    w_out = D("w_out", [1024, 1024])
    w_up = D("w_up", [1024, 4096])
    w_dn = D("w_dn", [4096, 1024])
    w_gate = D("w_gate", [1024, 1024])
    w_proj = D("w_proj", [256, 1024])
    cst = D("cst", [128, 512])
    gp = D("gp", [128, 20])
    gbc = D("gbc", [4, 128, 1024])
    neg = D("neg", [128, 16, 512])
    y = D("y", [2048, 1024], kind="ExternalOutput")
    dbg_out = {}

    arena = stack.enter_context(nc.sbuf_tensor("arena", [128, 212480 // 4], F32))
    ps = stack.enter_context(nc.psum_tensor("ps", [128, 4096], F32))

    def view(off, dtype, *shape):
        n = 1
        for s in shape:
            n *= s
        esz = 4 if dtype == F32 else 2
        nbytes = n * esz
        assert off % 4 == 0 and nbytes % 4 == 0
        ap = arena[:, off // 4:(off + nbytes) // 4]
        if dtype != F32:
            ap = ap.bitcast(dtype)
        if len(shape) == 2:
            ap = ap.rearrange("p (a b) -> p a b", a=shape[0])
        elif len(shape) == 3:
            ap = ap.rearrange("p (a b c) -> p a b c", a=shape[0], b=shape[1])
        return ap

    def bank(b):
        return ps[:, b * 512:(b + 1) * 512]

    pe = Eng(nc, nc.tensor, "pe", stack)
    act = Eng(nc, nc.scalar, "act", stack)
    vec = Eng(nc, nc.vector, "vec", stack)
    pool = Eng(nc, nc.gpsimd, "pool", stack)
    sp = Eng(nc, nc.sync, "sp", stack)
    for eng_ in (pe, act, vec, pool, sp):
        eng_.e.nop(cycle_cnt=12000)

    identf = view(0, F32, 128)
    cstb = view(512, BF16, 512)
    identb = cstb[:, 0:128]
    ntri = cstb[:, 128:256]
    nones = cstb[:, 256:384]
    bd = cstb[:, 384:512]
    gpt = view(1536, F32, 20)
    stat = view(1792, F32, 96)
    zl = view(2176, BF16, 64)
    aTh = view(2304, BF16, 8, 8)
    vhalo = view(2432, F32, 4, 8)
    cch = view(2560, F32, 4, 8)
    KT = view(4 * KB, BF16, 4, 4096)
    V = view(36 * KB, BF16, 32, 512)
    QT = view(68 * KB, BF16, 4, 2048)
    MIX = view(84 * KB, BF16, 8, 2048)
    W_IN = view(116 * KB, BF16, 8, 3072)
    XS = [view(164 * KB + s * 4 * KB, F32, 1024) for s in range(4)]
    AT = [view(180 * KB + s * 8 * KB, BF16, 8, 512) for s in range(2)]
    GB0 = view(196 * KB, F32, 1024)
    JUNK = view(200 * KB, BF16, 1024)
    cv = 100 * KB
    cb_sb = view(cv, F32, 512)
    cc_sb = view(cv + 2 * KB, F32, 512)
    cu_sb = view(cv + 4 * KB, F32, 512)
    vbuf = view(cv + 6 * KB, F32, 576)
    ybuf = view(cv + 6 * KB + 2304, F32, 512)
    ysq = view(cv + 6 * KB + 2304 + 2048, BF16, 512)
    tbuf = view(cv + 6 * KB + 2304 + 3072, F32, 512)
    rbuf = view(cv + 6 * KB + 2304 + 5120, F32, 512)
    NEG = view(116 * KB, BF16, 16, 512)
    LT = [view(132 * KB + s * KB, BF16, 512) for s in range(4)]
    ATT = [view(136 * KB + s * KB, BF16, 512) for s in range(4)]
    ET = [view(172 * KB + s * 2 * KB, F32, 512) for s in range(4)]
    PX = [view(180 * KB + s * 2 * KB, F32, 512) for s in range(2)]
    LSUM = [view(144 * KB + s * KB, BF16, 512) for s in range(2)]
    OSB = [view(146 * KB + s * 2 * KB, F32, 512) for s in range(2)]
    OSQ = [view(150 * KB + s * KB, BF16, 512) for s in range(2)]
    TB2 = [view(152 * KB + s * 2 * KB, F32, 512) for s in range(2)]
    W_OUT = view(156 * KB, BF16, 8, 1024)
    W_GATE = view(188 * KB, BF16, 8, 1024)
    H = [view(4 * KB + oc * 4 * KB, F32, 1024) for oc in range(16)]
    UT = [view(68 * KB + s * 8 * KB, BF16, 8, 512) for s in range(2)]
    MT = view(84 * KB, BF16, 8, 2048)
    UP = [view(116 * KB, BF16, 8, 1024), view(156 * KB, BF16, 8, 1024)]
    DN = [view(132 * KB, BF16, 8, 1024), view(172 * KB, BF16, 8, 1024)]
    W_PROJ = view(148 * KB, BF16, 2, 1024)
    GB1 = view(152 * KB, F32, 1024)
    A3 = [view(172 * KB + s * 4 * KB, F32, 1024) for s in range(2)]
    JUNK3 = view(180 * KB, BF16, 1024)
    RL = [view(204 * KB, F32, 512), view(152 * KB, F32, 512)]
    GB2 = view(116 * KB, F32, 1024)
    A5 = [view(120 * KB + s * 4 * KB, F32, 1024) for s in range(2)]
    GT = [view(128 * KB + s * 2 * KB, BF16, 8, 128) for s in range(2)]
    PS_ = [view(132 * KB + s * KB, F32, 256) for s in range(2)]
    PT = [view(134 * KB + s * 512, BF16, 2, 128) for s in range(2)]
    SG = [view(136 * KB + s * 2 * KB, F32, 512) for s in range(2)]
    OUTB = [view(156 * KB + s * 4 * KB, F32, 1024) for s in range(2)]
    JUNK5 = view(164 * KB, BF16, 1024)

    free = {}

    def fr(k):
        return free.get(k)

    d_c = DSem(nc, "c", stack)
    tk_c1 = d_c.dma(sp, identf, cst[:, 0:128])
    tk_c2 = d_c.dma(sp, gpt, gp[:, :])
    tk_c3 = d_c.dma(sp, GB0, gbc[0])
    d_cb = DSem(nc, "cb", stack)
    tk_cb = d_cb.dma(pool, cstb, cst[:, :])
    tk_zl = pool.tick(pool.e.memset(zl, 0.0))
    d_wkv = DSem(nc, "wkv", stack)
    d_win = DSem(nc, "win", stack)
    for kc in range(8):
        tk_wkv = d_wkv.dma(pool, W_IN[:, kc, 2048:3072], w_in[kc * 128:(kc + 1) * 128, 2048:3072])
    for c0 in (0, 1024):
        for kc in range(8):
            tk_win = d_win.dma(pool, W_IN[:, kc, c0:c0 + 1024], w_in[kc * 128:(kc + 1) * 128, c0:c0 + 1024])
    tk_consts = [tk_c3, tk_cb]

    stat_n = [0]

    def stat3():
        i = stat_n[0] % 30
        stat_n[0] += 1
        return stat[:, 3 * i:3 * i + 1], stat[:, 3 * i + 1:3 * i + 2], stat[:, 3 * i + 2:3 * i + 3]

    d_xs = [DSem(nc, "xs%d" % s, stack) for s in range(4)]
    xs_n = [0]
    tp_n = [0]

    def frontA(rows_ap, npart, gb, junk):
        s = xs_n[0] % 4
        xs_n[0] += 1
        xt = XS[s][0:npart, :]
        sp.wait(fr(("xs", s)))
        ld = d_xs[s].dma(sp, xt, rows_ap)
        ssq, tt, rs = stat3()
        ssq, tt, rs = ssq[0:npart, :], tt[0:npart, :], rs[0:npart, :]
        act.wait(ld, tk_consts)
        a1 = act.tick(act.e.activation(out=junk[0:npart, :], in_=xt, func=AF.Square, accum_out=ssq))
        act.wait(a1)
        a2 = act.tick(act.e.activation(out=tt, in_=ssq, func=AF.Ln, scale=1.0 / 1024.0, bias=EPS))
        act.wait(a2)
        a3 = act.tick(act.e.activation(out=rs, in_=tt, func=AF.Exp, scale=-0.5))
        vec.wait(a3, ld, tk_consts)
        v1 = vec.tick(vec.e.scalar_tensor_tensor(out=xt, in0=xt, scalar=rs, in1=gb[0:npart, :],
                                                 op0=ALU.mult, op1=ALU.mult))
        return dict(s=s, xt=xt, v1=v1, npart=npart)

    def frontB(c, dst, extra_dst_wait=None):
        s, xt, v1, npart = c["s"], c["xt"], c["v1"], c["npart"]
        pr = tp_n[0] % 2
        tp_n[0] += 1
        tpb = ps[:, pr * 1024:(pr + 1) * 1024]
        pe.wait(v1, fr(("tp", pr)), tk_consts)
        for kc in range(8):
            ins = pe.e.transpose(out=tpb[:, kc * npart:(kc + 1) * npart], in_=xt[:, kc * 128:(kc + 1) * 128],
                                 identity=identf[0:npart, 0:npart])
        p1 = pe.tick(ins)
        free[("xs", s)] = p1
        act.wait(p1, extra_dst_wait)
        if npart == 128:
            act.tick(act.e.activation(out=dst[:, 0:4, :], in_=tpb[:, 0:512].rearrange("p (a b) -> p a b", a=4),
                                      func=AF.Copy))
            e1 = act.tick(act.e.activation(out=dst[:, 4:8, :],
                                           in_=tpb[:, 512:1024].rearrange("p (a b) -> p a b", a=4), func=AF.Copy))
        else:
            e1 = act.tick(act.e.activation(out=dst, in_=tpb[:, 0:8 * npart].rearrange("p (a b) -> p a b", a=8),
                                           func=AF.Copy))
        free[("tp", pr)] = e1
        return e1

    def front(rows_ap, npart, gb, dst, junk, extra_dst_wait=None):
        return frontB(frontA(rows_ap, npart, gb, junk), dst, extra_dst_wait)

    ring_n = [0]

    def ring_bank():
        b = 4 + ring_n[0] % 4
        ring_n[0] += 1
        return b

    def mm_group(b, pairs, cols=None):
        out = bank(b) if cols is None else bank(b)[:, cols[0]:cols[1]]
        pe.wait(fr(("bank", b)))
        n = len(pairs)
        for i, (l, r_) in enumerate(pairs):
            ins = pe.e.matmul(out, l, r_, start=(i == 0), stop=(i == n - 1))
        return pe.tick(ins)

    def kv_after(j, e1):
        buf = j % 2
        pe.wait(e1, tk_wkv)
        for m in range(4):
            b = ring_bank()
            t = mm_group(b, [(W_IN[:, kc, 2048 + m * 128:2048 + (m + 1) * 128], AT[buf][:, kc, :]) for kc in range(8)])
            vec.wait(t)
            free[("bank", b)] = vec.tick(vec.e.tensor_copy(out=KT[:, m, j * 512:(j + 1) * 512], in_=bank(b)))
        for cc in range(4):
            b = ring_bank()
            t = mm_group(b, [(AT[buf][:, kc, cc * 128:(cc + 1) * 128], W_IN[:, kc, 2560:3072]) for kc in range(8)])
            vec.wait(t)
            free[("bank", b)] = vec.tick(vec.e.tensor_copy(out=V[:, j * 4 + cc, :], in_=bank(b)))
        free[("at", buf)] = t

    halo_tk = {}

    def halo_block():
        e1 = front(xh[:, :], 8, GB0, aTh, JUNK)
        pe.wait(e1, tk_win)
        hb = ring_bank()
        pe.wait(fr(("bank", hb)))
        for m8 in range(8):
            col0 = 512 + m8 * 128
            for kc in range(8):
                ins = pe.e.matmul(bank(hb)[:, m8 * 8:(m8 + 1) * 8], W_IN[:, kc, col0:col0 + 128], aTh[:, kc, :],
                                  start=(kc == 0), stop=(kc == 7))
        t = pe.tick(ins)
        act.wait(t)
        a = act.tick(act.e.activation(out=cch, in_=bank(hb)[:, 0:32].rearrange("p (a b) -> p a b", a=4), func=AF.Copy))
        vec.wait(a)
        free[("bank", hb)] = vec.tick(vec.e.tensor_tensor(out=vhalo, in0=cch,
                                                          in1=bank(hb)[:, 32:64].rearrange("p (a b) -> p a b", a=4),
                                                          op=ALU.mult))
        halo_tk["v"] = free[("bank", hb)]

    def own_after(i, e1):
        buf = i % 2
        if i == 0:
            halo_block()
        pe.wait(e1)
        tsl = slice(i * 512, (i + 1) * 512)
        for c in range(4):
            tks = []
            bs = []
            for col0 in (c * 128, 512 + c * 128, 1024 + c * 128):
                b = ring_bank()
                bs.append(b)
                tks.append(mm_group(b, [(W_IN[:, kc, col0:col0 + 128], AT[buf][:, kc, :]) for kc in range(8)]))
            act.wait(tks[0], fr("cb_sb"))
            free[("bank", bs[0])] = a_cb = act.tick(act.e.activation(out=cb_sb, in_=bank(bs[0]), func=AF.Copy))
            act.wait(tks[1], fr("cc_sb"))
            free[("bank", bs[1])] = a_cc = act.tick(act.e.activation(out=cc_sb, in_=bank(bs[1]), func=AF.Copy))
            act.wait(tks[2], fr("cu_sb"))
            free[("bank", bs[2])] = a_cu = act.tick(act.e.activation(out=cu_sb, in_=bank(bs[2]), func=AF.Copy))
            pool.wait(a_cc, a_cu, fr("vbuf"), halo_tk["v"])
            pool.tick(pool.e.tensor_copy(out=vbuf[:, 0:2], in_=vhalo[:, c, 2 * i:2 * i + 2]))
            g1 = pool.tick(pool.e.tensor_tensor(out=vbuf[:, 2:514], in0=cc_sb, in1=cu_sb, op=ALU.mult))
            free["cc_sb"] = g1
            free["cu_sb"] = g1
            w0 = gpt[:, 8 + c * 3:9 + c * 3]
            w1 = gpt[:, 9 + c * 3:10 + c * 3]
            w2 = gpt[:, 10 + c * 3:11 + c * 3]
            vec.wait(g1, fr("ybuf"), tk_consts)
            d1 = vec.tick(vec.e.tensor_scalar(out=ybuf, in0=vbuf[:, 0:512], scalar1=w0, scalar2=None, op0=ALU.mult))
            vec.wait(d1)
            d2 = vec.tick(vec.e.scalar_tensor_tensor(out=ybuf, in0=vbuf[:, 1:513], scalar=w1, in1=ybuf,
                                                     op0=ALU.mult, op1=ALU.add))
            vec.wait(d2)
            d3 = vec.tick(vec.e.scalar_tensor_tensor(out=ybuf, in0=vbuf[:, 2:514], scalar=w2, in1=ybuf,
                                                     op0=ALU.mult, op1=ALU.add))
            free["vbuf"] = d3
            vec.wait(d3, a_cb)
            d4 = vec.tick(vec.e.tensor_tensor(out=cb_sb, in0=ybuf, in1=cb_sb, op=ALU.mult))
            free["ybuf"] = d4
            act.wait(d4, fr("ysq"))
            a_sq = act.tick(act.e.activation(out=ysq, in_=cb_sb, func=AF.Square))
            b = ring_bank()
            pe.wait(a_sq, tk_consts)
            t_ms = mm_group(b, [(bd, ysq)])
            free["ysq"] = t_ms
            act.wait(t_ms, fr("tbuf"))
            a_ln = act.tick(act.e.activation(out=tbuf, in_=bank(b), func=AF.Ln, bias=EPS))
            free[("bank", b)] = a_ln
            act.wait(a_ln, fr("rbuf"))
            a_rs = act.tick(act.e.activation(out=rbuf, in_=tbuf, func=AF.Exp, scale=-0.5))
            free["tbuf"] = a_rs
            vec.wait(a_rs)
            d5 = vec.tick(vec.e.scalar_tensor_tensor(out=MIX[:, c, tsl], in0=cb_sb, scalar=gpt[:, c:c + 1], in1=rbuf,
                                                     op0=ALU.mult, op1=ALU.mult))
            free["cb_sb"] = d5
            free["rbuf"] = d5
        for m in range(4):
            b = ring_bank()
            t = mm_group(b, [(W_IN[:, kc, 1536 + m * 128:1536 + (m + 1) * 128], AT[buf][:, kc, :]) for kc in range(8)])
            vec.wait(t)
            free[("bank", b)] = vec.tick(vec.e.tensor_scalar(out=QT[:, m, tsl], in0=bank(b), scalar1=0.125,
                                                             scalar2=None, op0=ALU.mult))
        free[("at", buf)] = t

    chunks = []
    for j in range(8):
        for cc in range(4):
            r0 = (j * 4 + cc) * 128
            chunks.append(dict(rows=xa[r0:r0 + 128, :], buf=j % 2, cc=cc,
                               after=(lambda e1, j=j: kv_after(j, e1)) if cc == 3 else None))
    for i in range(4):
        for cc in range(4):
            r0 = (i * 4 + cc) * 128
            chunks.append(dict(rows=xo[r0:r0 + 128, :], buf=i % 2, cc=cc,
                               after=(lambda e1, i=i: own_after(i, e1)) if cc == 3 else None))
    LOOK = 2
    ctxs = {}
    for n in range(min(LOOK, len(chunks))):
        ctxs[n] = frontA(chunks[n]["rows"], 128, GB0, JUNK)
    for n, ch in enumerate(chunks):
        if n + LOOK < len(chunks):
            ctxs[n + LOOK] = frontA(chunks[n + LOOK]["rows"], 128, GB0, JUNK)
        e1 = frontB(ctxs.pop(n), AT[ch["buf"]][:, :, ch["cc"] * 128:(ch["cc"] + 1) * 128], fr(("at", ch["buf"])))
        if ch["after"] is not None:
            ch["after"](e1)


    p1_done = [pe.last, act.last, vec.last, pool.last]

    d_neg = DSem(nc, "neg", stack)
    d_wo = DSem(nc, "wo", stack)
    d_wg = DSem(nc, "wg", stack)
    pool.wait(p1_done)
    for g in range(4):
        tk_neg = d_neg.dma(pool, NEG[:, g * 4:(g + 1) * 4, :], neg[:, g * 4:(g + 1) * 4, :])
    for kc in range(8):
        tk_wo = d_wo.dma(pool, W_OUT[:, kc, :], w_out[kc * 128:(kc + 1) * 128, :])
    for kc in range(8):
        tk_wg = d_wg.dma(pool, W_GATE[:, kc, :], w_gate[kc * 128:(kc + 1) * 128, :])

    units = []
    for i in range(4):
        ntile = 2 * i + 2
        for h in range(8):
            ul = []
            for tix in range(ntile - 1, -1, -1):
                for b in range(3, -1, -1):
                    top = (tix == ntile - 1)
                    second = (tix == ntile - 2)
                    c0 = 128 * b if top else 0
                    mk = None
                    if top:
                        mk = NEG[:, (i % 2) * 8 + b, :]
                    elif second:
                        mk = NEG[:, (i % 2) * 8 + 4 + b, :]
                    ul.append(dict(i=i, h=h, kb=tix * 4 + b, c0=c0, mk=mk))
            for u, d in enumerate(ul):
                d["u"] = u
                d["first"] = (u == 0)
                d["last"] = (u == len(ul) - 1)
            units.extend(ul)
    for k, d in enumerate(units):
        d["k"] = k

    SB = [0, 1]
    N_FILL = 2
    SB2 = [6, 7]
    OB = [3, 4]
    MSB = 5
    deferred = []

    def st_z(d):
        k, i, h, c0 = d["k"], d["i"], d["h"], d["c0"]
        hc, hp = h // 2, (h % 2) * 64
        sb = SB[k % 2]
        out = bank(sb)[:, c0:512]
        pe.wait(fr(("bank", sb)), p1_done)
        kk = d["kb"] * 128
        has_mk = d["mk"] is not None
        ins = pe.e.matmul(out, KT[hp:hp + 64, hc, kk:kk + 128], QT[hp:hp + 64, hc, i * 512 + c0:(i + 1) * 512],
                          start=True, stop=not has_mk)
        if has_mk:
            pe.wait(tk_neg)
            ins = pe.e.matmul(out, identb, d["mk"][:, c0:512], start=False, stop=True)
        d["t_z"] = pe.tick(ins)

    def st_e1(d):
        k, c0 = d["k"], d["c0"]
        sb = SB[k % 2]
        act.wait(d["t_z"], fr(("et", k % 4)))
        d["t_e1"] = act.tick(act.e.activation(out=ET[k % 4][:, c0:512], in_=bank(sb)[:, c0:512], func=AF.Exp))
        free[("bank", sb)] = d["t_e1"]

    def st_ln(d):
        k, c0 = d["k"], d["c0"]
        act.wait(d["t_e1"], fr(("lt", k % 4)))
        d["t_ln"] = act.tick(act.e.activation(out=LT[k % 4][:, c0:512], in_=ET[k % 4][:, c0:512], func=AF.Ln, bias=1.0))

    def st_tc(d):
        k, c0, h = d["k"], d["c0"], d["h"]
        sb = SB2[k % 2]
        out = bank(sb)[:, c0:512]
        for _ in range(N_FILL):
            pe.e.matmul(bank(2), identb, cstb, start=True, stop=True)
        pe.wait(d["t_ln"], tk_consts, fr(("bank", sb)))
        ins = pe.e.matmul(out, ntri, LT[k % 4][:, c0:512], start=True, stop=d["first"])
        if not d["first"]:
            pe.wait(fr(("lsum_w", h % 2)))
            ins = pe.e.matmul(out, nones, LSUM[h % 2][:, c0:512], start=False, stop=True)
        d["t_tc"] = pe.tick(ins)

    def st_add(d):
        k, c0, h = d["k"], d["c0"], d["h"]
        if d["first"]:
            pool.wait(fr(("lsum_r", h % 2)))
            free[("lsum_w", h % 2)] = pool.tick(pool.e.memset(LSUM[h % 2], 0.0))
        if d["last"]:
            free[("lt", k % 4)] = d["t_tc"]
            free[("lsum_r", h % 2)] = d["t_tc"]
            return
        pool.wait(d["t_ln"], d["t_tc"], fr(("lsum_w", h % 2)))
        t = pool.tick(pool.e.tensor_tensor(out=LSUM[h % 2][:, c0:512], in0=LSUM[h % 2][:, c0:512],
                                           in1=LT[k % 4][:, c0:512], op=ALU.add))
        free[("lsum_w", h % 2)] = t
        free[("lt", k % 4)] = [t, d["t_tc"]]

    def st_e3(d):
        k, c0 = d["k"], d["c0"]
        sb = SB2[k % 2]
        act.wait(d["t_tc"], fr(("px", k % 2)))
        d["t_e3"] = act.tick(act.e.activation(out=PX[k % 2][:, c0:512], in_=bank(sb)[:, c0:512], func=AF.Exp))
        free[("bank", sb)] = d["t_e3"]
        vec.wait(d["t_e3"], d["t_e1"], fr(("att", k % 4)))
        d["t_m"] = vec.tick(vec.e.tensor_tensor(out=ATT[k % 4][:, c0:512], in0=ET[k % 4][:, c0:512],
                                                in1=PX[k % 2][:, c0:512], op=ALU.mult))
        free[("px", k % 2)] = d["t_m"]
        free[("et", k % 4)] = [d["t_m"], d["t_ln"]]

    def st_av(d):
        k, c0, h, i = d["k"], d["c0"], d["h"], d["i"]
        pair = (i * 8 + h) // 2
        ob = OB[pair % 2]
        hp = (h % 2) * 64
        if d["first"]:
            if h % 2 == 0:
                pe.wait(fr(("bank", ob)))
            pe.wait(tk_zl, tk_consts)
            pe.e.matmul(bank(ob)[hp:hp + 64, :], zl, cstb, start=True, stop=False)
        pe.wait(d["t_m"])
        ins = pe.e.matmul(bank(ob)[hp:hp + 64, c0:512], V[:, d["kb"], h * 64:(h + 1) * 64], ATT[k % 4][:, c0:512],
                          start=False, stop=d["last"])
        t = pe.tick(ins)
        free[("att", k % 4)] = t
        if d["last"] and h % 2 == 1:
            fin_pair(i, h // 2, pair, ob, t)

    def fin_pair(i, hpair, pair, ob, t_av):
        s2 = pair % 2
        tsl = slice(i * 512, (i + 1) * 512)

        def f1():
            vec.wait(t_av, fr(("osb", s2)))
            c1 = vec.tick(vec.e.tensor_copy(out=OSB[s2], in_=bank(ob)))
            free[("bank", ob)] = c1
            pool.wait(c1, fr(("osq", s2)))
            q1 = pool.tick(pool.e.tensor_tensor(out=OSQ[s2], in0=OSB[s2], in1=OSB[s2], op=ALU.mult))

            def f2():
                pe.wait(q1, fr(("bank", MSB)))
                ins = pe.e.matmul(bank(MSB), bd, OSQ[s2], start=True, stop=True)
                t_ms = pe.tick(ins)
                free[("osq", s2)] = t_ms

                def f3():
                    act.wait(t_ms, fr(("tb2", s2)))
                    a_ln = act.tick(act.e.activation(out=TB2[s2], in_=bank(MSB), func=AF.Ln, bias=EPS))
                    free[("bank", MSB)] = a_ln
                    act.wait(a_ln)
                    a_rs = act.tick(act.e.activation(out=TB2[s2], in_=TB2[s2], func=AF.Exp, scale=-0.5))
                    vec.wait(a_rs)
                    d5 = vec.tick(vec.e.scalar_tensor_tensor(out=MIX[:, 4 + hpair, tsl], in0=OSB[s2],
                                                             scalar=gpt[:, 4 + hpair:5 + hpair], in1=TB2[s2],
                                                             op0=ALU.mult, op1=ALU.mult))
                    free[("osb", s2)] = d5
                    free[("tb2", s2)] = d5
                deferred.append([1, f3])
            deferred.append([1, f2])
        deferred.append([0, f1])

    def run_deferred(flush=False):
        while True:
            ready = [x for x in deferred if x[0] <= 0]
            if not ready:
                if flush and deferred:
                    for x in deferred:
                        x[0] -= 1
                    continue
                break
            for x in ready:
                deferred.remove(x)
                x[1]()
        for x in deferred:
            x[0] -= 1

    NU = len(units)
    st_z(units[0])
    for k in range(NU + 1):
        if k + 1 < NU:
            st_z(units[k + 1])
        if k < NU:
            st_e1(units[k])
        if k >= 1:
            st_e3(units[k - 1])
        if k < NU:
            st_ln(units[k])
            st_tc(units[k])
            st_add(units[k])
        if k >= 1:
            st_av(units[k - 1])
        run_deferred()
    run_deferred(flush=True)

    p2_done = [pe.last, act.last, vec.last, pool.last]

    d_h = [DSem(nc, "h%d" % s, stack) for s in range(16)]
    d_wp = DSem(nc, "wp", stack)
    d_g1 = DSem(nc, "g1", stack)
    d_ml = [DSem(nc, "ml%d" % s, stack) for s in range(2)]

    def load_quarter(qq, bufq, waits):
        pool.wait(waits)
        for kc in range(8):
            d_ml[bufq].dma(pool, UP[bufq][:, kc, :], w_up[kc * 128:(kc + 1) * 128, qq * 1024:(qq + 1) * 1024])
        for fc in range(8):
            tk = d_ml[bufq].dma(pool, DN[bufq][:, fc, :], w_dn[qq * 1024 + fc * 128:qq * 1024 + (fc + 1) * 128, :])
        return tk

    pool.wait(p2_done)
    for kc in range(2):
        tk_wp = d_wp.dma(pool, W_PROJ[:, kc, :], w_proj[kc * 128:(kc + 1) * 128, :])
    tk_q = [None] * 4
    tk_q[0] = load_quarter(0, 0, None)
    sp.wait(p2_done)
    tk_g1 = d_g1.dma(sp, GB1, gbc[1])
    tk_h = []
    for oc in range(16):
        tk_h.append(d_h[oc].dma(sp, H[oc], xo[oc * 128:(oc + 1) * 128, :]))
    t_h1 = [None] * 16
    for oc in range(16):
        for hf in range(2):
            b = ring_bank()
            pe.wait(tk_wo, p2_done)
            t = mm_group(b, [(MIX[:, kc, oc * 128:(oc + 1) * 128], W_OUT[:, kc, hf * 512:(hf + 1) * 512])
                             for kc in range(8)])
            vec.wait(t, tk_h[oc])
            hs = H[oc][:, hf * 512:(hf + 1) * 512]
            t_h1[oc] = free[("bank", b)] = vec.tick(vec.e.tensor_tensor(out=hs, in0=hs, in1=bank(b), op=ALU.add))
    p3a_pe = pe.last

    a3_n = [0]

    def front3A(src, gb, junk, abufs, src_tk, gb_tk):
        s = a3_n[0] % 2
        a3_n[0] += 1
        xt = abufs[s]
        ssq, tt, rs = stat3()
        act.wait(src_tk)
        a1 = act.tick(act.e.activation(out=junk, in_=src, func=AF.Square, accum_out=ssq))
        act.wait(a1)
        a2 = act.tick(act.e.activation(out=tt, in_=ssq, func=AF.Ln, scale=1.0 / 1024.0, bias=EPS))
        act.wait(a2)
        a3 = act.tick(act.e.activation(out=rs, in_=tt, func=AF.Exp, scale=-0.5))
        vec.wait(a3, src_tk, fr(("a3", id(abufs), s)), gb_tk)
        v1 = vec.tick(vec.e.scalar_tensor_tensor(out=xt, in0=src, scalar=rs, in1=gb, op0=ALU.mult, op1=ALU.mult))
        return dict(s=s, xt=xt, v1=v1, key=("a3", id(abufs), s))

    def front3B(c, dst, dst_wait):
        xt, v1 = c["xt"], c["v1"]
        pr = tp_n[0] % 2
        tp_n[0] += 1
        tpb = ps[:, pr * 1024:(pr + 1) * 1024]
        pe.wait(v1, fr(("tp", pr)))
        for kc in range(8):
            ins = pe.e.transpose(out=tpb[:, kc * 128:(kc + 1) * 128], in_=xt[:, kc * 128:(kc + 1) * 128],
                                 identity=identf)
        p1 = pe.tick(ins)
        free[c["key"]] = p1
        act.wait(p1, dst_wait)
        act.tick(act.e.activation(out=dst[:, 0:4, :], in_=tpb[:, 0:512].rearrange("p (a b) -> p a b", a=4),
                                  func=AF.Copy))
        e1 = act.tick(act.e.activation(out=dst[:, 4:8, :], in_=tpb[:, 512:1024].rearrange("p (a b) -> p a b", a=4),
                                       func=AF.Copy))
        free[("tp", pr)] = e1
        return e1

    c3 = front3A(H[0], GB1, JUNK3, A3, t_h1[0], tk_g1)
    for oc in range(16):
        nx = front3A(H[oc + 1], GB1, JUNK3, A3, t_h1[oc + 1], tk_g1) if oc + 1 < 16 else None
        t_mt = front3B(c3, MT[:, :, oc * 128:(oc + 1) * 128], p3a_pe)
        c3 = nx
    mt_done = [pe.last, act.last, vec.last]

    ut_n = [0]
    rl_n = [0]
    for qq in range(4):
        bq = qq % 2
        if qq + 1 < 4:
            tk_q[qq + 1] = load_quarter(qq + 1, 1 - bq, mt_done if qq == 0 else fr(("mlq", 1 - bq)))
        for i in range(4):
            ub = ut_n[0] % 2
            ut_n[0] += 1
            pe.wait(tk_q[qq], t_mt)
            for fc in range(8):
                b = ring_bank()
                t = mm_group(b, [(UP[bq][:, kc, fc * 128:(fc + 1) * 128], MT[:, kc, i * 512:(i + 1) * 512])
                                 for kc in range(8)])
                rs_ = rl_n[0] % 2
                rl_n[0] += 1
                act.wait(t, fr(("rl", rs_)))
                a_r = act.tick(act.e.activation(out=RL[rs_], in_=bank(b), func=AF.Relu))
                free[("bank", b)] = a_r
                pool.wait(a_r, fr(("ut", ub)) if fc == 0 else None)
                g_ = pool.tick(pool.e.tensor_tensor(out=UT[ub][:, fc, :], in0=RL[rs_], in1=RL[rs_], op=ALU.mult))
                free[("rl", rs_)] = g_
            pe.wait(g_)
            for cc in range(4):
                oc = i * 4 + cc
                for hf in range(2):
                    b = ring_bank()
                    t = mm_group(b, [(UT[ub][:, fc, cc * 128:(cc + 1) * 128], DN[bq][:, fc, hf * 512:(hf + 1) * 512])
                                     for fc in range(8)])
                    vec.wait(t, t_h1[oc])
                    hs = H[oc][:, hf * 512:(hf + 1) * 512]
                    t_h1[oc] = free[("bank", b)] = vec.tick(vec.e.tensor_tensor(out=hs, in0=hs, in1=bank(b), op=ALU.add))
            free[("ut", ub)] = t
        free[("mlq", bq)] = pe.last
    p3b_done = [pe.last, act.last, vec.last, pool.last]

    d_g2 = DSem(nc, "g2", stack)
    d_g3 = DSem(nc, "g3", stack)
    d_p = [DSem(nc, "p%d" % s, stack) for s in range(2)]
    d_o = [DSem(nc, "o%d" % s, stack) for s in range(2)]
    SG4 = [view(136 * KB + s * 2 * KB, F32, 512) for s in range(4)]
    sp.wait(p3b_done)
    act.wait(p3b_done)
    vec.wait(p3b_done)
    pe.wait(p3b_done)
    tk_g2 = d_g2.dma(sp, GB2, gbc[2])
    tk_g3 = d_g3.dma(sp, GB1, gbc[3])
    o_tk = [None, None]

    def pleA(oc):
        s2 = oc % 2
        c = front3A(H[oc], GB2, JUNK5, A5, t_h1[oc], tk_g2)
        sp.wait(fr(("ps", s2)))
        c["tk_p"] = d_p[s2].dma(sp, PS_[s2], po[oc * 128:(oc + 1) * 128, :])
        return c

    def pleB(oc, c):
        s2 = oc % 2
        e1 = front3B(c, GT[s2], fr(("gt", s2)))
        b = ring_bank()
        pe.wait(c["tk_p"], fr(("bank", b)))
        for k2 in range(2):
            ins = pe.e.transpose(out=bank(b)[:, k2 * 128:(k2 + 1) * 128], in_=PS_[s2][:, k2 * 128:(k2 + 1) * 128],
                                 identity=identf)
        t = pe.tick(ins)
        free[("ps", s2)] = t
        vec.wait(t, fr(("pt", s2)))
        free[("bank", b)] = c_pt = vec.tick(vec.e.tensor_copy(out=PT[s2],
                                                             in_=bank(b)[:, 0:256].rearrange("p (a b) -> p a b", a=2)))
        pe.wait(e1, c_pt, tk_wg, tk_wp)
        for hf in range(2):
            bg = ring_bank()
            t_g = mm_group(bg, [(GT[s2][:, kc, :], W_GATE[:, kc, hf * 512:(hf + 1) * 512]) for kc in range(8)])
            bp = ring_bank()
            t_p = mm_group(bp, [(PT[s2][:, k2, :], W_PROJ[:, k2, hf * 512:(hf + 1) * 512]) for k2 in range(2)])
            sg = SG4[s2 * 2 + hf]
            act.wait(t_g, fr(("sg", s2 * 2 + hf)))
            a_e = act.tick(act.e.activation(out=sg, in_=bank(bg), func=AF.Exp, scale=-1.0))
            free[("bank", bg)] = a_e
            act.wait(a_e)
            a_l = act.tick(act.e.activation(out=sg, in_=sg, func=AF.Ln, bias=1.0))
            act.wait(a_l)
            a_s = act.tick(act.e.activation(out=sg, in_=sg, func=AF.Exp, scale=-1.0))
            vec.wait(a_s, t_p)
            v3 = vec.tick(vec.e.tensor_tensor(out=sg, in0=sg, in1=bank(bp), op=ALU.mult))
            free[("bank", bp)] = v3
            hs = H[oc][:, hf * 512:(hf + 1) * 512]
            vec.wait(v3, t_h1[oc])
            v4 = vec.tick(vec.e.tensor_tensor(out=hs, in0=hs, in1=sg, op=ALU.add))
            free[("sg", s2 * 2 + hf)] = v4
        free[("gt", s2)] = t_g
        free[("pt", s2)] = t_p
        ssq, tt, rs = stat3()
        act.wait(v4)
        a1 = act.tick(act.e.activation(out=JUNK5, in_=H[oc], func=AF.Square, accum_out=ssq))
        act.wait(a1)
        a2 = act.tick(act.e.activation(out=tt, in_=ssq, func=AF.Ln, scale=1.0 / 1024.0, bias=EPS))
        act.wait(a2)
        a3 = act.tick(act.e.activation(out=rs, in_=tt, func=AF.Exp, scale=-0.5))
        vec.wait(a3, o_tk[s2], tk_g3)
        v5 = vec.tick(vec.e.scalar_tensor_tensor(out=OUTB[s2], in0=H[oc], scalar=rs, in1=GB1,
                                                 op0=ALU.mult, op1=ALU.mult))
        sp.wait(v5)
        o_tk[s2] = d_o[s2].dma(sp, y[oc * 128:(oc + 1) * 128, :], OUTB[s2])

    cP = pleA(0)
    for oc in range(16):
        nx = pleA(oc + 1) if oc + 1 < 16 else None
        pleB(oc, cP)
        cP = nx
    sp.wait(o_tk[0], o_tk[1])
    stack.close()
    return nc


def _consts():
    c = np.zeros((128, 512), np.float32)
    j = np.arange(128)[:, None]
    s = np.arange(128)[None, :]
    c[:, 0:128] = (j == s)
    c[:, 128:256] = np.where(j >= s, -1.0, 0.0)
    c[:, 256:384] = -1.0
    c[:, 384:512] = np.where((j // 64) == (s // 64), 1.0 / 64.0, 0.0)
    return c


def _neg(r):
    out = np.zeros((128, 16, 512), np.float32)
    s = np.arange(128)[:, None]
    t = np.arange(512)[None, :]
    for e in range(2):
        typ = "A" if (e + r) % 2 == 1 else "B"
        for b in range(4):
            tri = np.where((128 * b + s) < t, 0.0, NEGV).astype(np.float32)
            if typ == "A":
                out[:, e * 8 + b, :] = tri
                out[:, e * 8 + 4 + b, :] = 0.0
            else:
                out[:, e * 8 + b, :] = NEGV
                out[:, e * 8 + 4 + b, :] = tri
    return out


_NC_CACHE = {}


def kernel(x, p, g_mix, w_in, conv_w, g_conv_out, g_attn_out, w_out, g_mlp, w_up, w_down,
           g_ple, w_ple_gate, w_ple_proj, g_final):
    f = lambda a: np.ascontiguousarray(np.asarray(a, dtype=np.float32))
    x, p = f(x), f(p)
    g_mix, g_mlp, g_ple, g_final = f(g_mix), f(g_mlp), f(g_ple), f(g_final)
    conv_w, g_conv_out, g_attn_out = f(conv_w), f(g_conv_out), f(g_attn_out)
    wi, wo, wu, wd, wg, wp = f(w_in)[0], f(w_out)[0], f(w_up)[0], f(w_down)[0], f(w_ple_gate)[0], f(w_ple_proj)[0]

    gp = np.zeros((128, 20), np.float32)
    gp[:, 0:4] = g_conv_out[0].reshape(4, 128).T
    gp[:, 4:8] = g_attn_out[0].reshape(4, 128).T
    for c in range(4):
        for k in range(3):
            gp[:, 8 + c * 3 + k] = conv_w[0, k, c * 128:(c + 1) * 128]
    gbc = np.stack([np.broadcast_to(v, (128, 1024)) for v in (g_mix[0], g_mlp[0], g_ple[0], g_final)]).astype(np.float32)
    gbc = np.ascontiguousarray(gbc)
    cst = _consts()

    in_maps = []
    for c in range(8):
        b, r = c // 2, c % 2
        tiles = OWN[r]
        xo = np.concatenate([x[b, t * 512:(t + 1) * 512] for t in tiles], axis=0)
        xh = np.zeros((8, 1024), np.float32)
        for i, t in enumerate(tiles):
            if t > 0:
                xh[2 * i:2 * i + 2] = x[b, t * 512 - 2:t * 512]
        po = np.concatenate([p[0, b, t * 512:(t + 1) * 512] for t in tiles], axis=0)
        in_maps.append({
            "xa": x[b], "xo": np.ascontiguousarray(xo), "xh": xh, "po": np.ascontiguousarray(po),
            "w_in": wi, "w_out": wo, "w_up": wu, "w_dn": wd, "w_gate": wg, "w_proj": wp,
            "cst": cst, "gp": gp, "gbc": gbc, "neg": _neg(r),
        })
    if "nc" not in _NC_CACHE:
        _NC_CACHE["nc"] = build_nc()
    nc = _NC_CACHE["nc"]
    res = run_bass_kernel_spmd(nc, in_maps, core_ids=list(range(8)))
    out = np.zeros((4, 4096, 1024), np.float32)
    for c in range(8):
        b, r = c // 2, c % 2
        yc = np.asarray(res.results[c]["y"], dtype=np.float32)
        for i, t in enumerate(OWN[r]):
            out[b, t * 512:(t + 1) * 512] = yc[i * 512:(i + 1) * 512]
    return out
```
